# Optimizing a Trainium2 kernel written in Bass

```python
import math
import jax, jax.numpy as jnp
from jax import lax
import numpy as np

D_MODEL = 1024
BATCH = 8
SEQ = 2048
DEPTH = 2

N_MIXERS = 2
HEAD_DIM = 128
N_MAIN_HEADS = 12
N_MEM_HEADS = 4
N_MEM = 256
INNER = (N_MAIN_HEADS + N_MEM_HEADS) * HEAD_DIM
MAIN_W = N_MAIN_HEADS * HEAD_DIM
MEM_W = N_MEM_HEADS * HEAD_DIM
RET_KEY_DIM = HEAD_DIM // 2
BLOCK_Q = 128
RET_CHUNK = 128
ROPE_BASE = 10000.0
EPS = 1e-6
NEG = -1e30
FOX_IN = 3 * MAIN_W + N_MAIN_HEADS + MEM_W + INNER
RET_IN = 2 * N_MAIN_HEADS * RET_KEY_DIM + MAIN_W + MEM_W + INNER
N_FOX = len(range(0, DEPTH, N_MIXERS))
N_RET = len(range(1, DEPTH, N_MIXERS))

kernel_name = "fox_retention_interleaved_hybrid"


def rmsnorm(x, g):
    xf = x.astype(jnp.float32)
    y = xf * lax.rsqrt(jnp.mean(xf * xf, axis=-1, keepdims=True) + EPS)
    return (y * g.astype(jnp.float32)).astype(x.dtype)


def split_cols(t, sizes):
    idx = list(np.cumsum(sizes)[:-1])
    return jnp.split(t, idx, axis=-1)


def rotary(t, pos):
    half = t.shape[-1] // 2
    inv = 1.0 / (ROPE_BASE ** (jnp.arange(half, dtype=jnp.float32) / half))
    ang = pos[:, None] * inv[None, :]
    cos = jnp.cos(ang)[None, :, None, :].astype(t.dtype)
    sin = jnp.sin(ang)[None, :, None, :].astype(t.dtype)
    t1, t2 = t[..., :half], t[..., half:]
    return jnp.concatenate([t1 * cos - t2 * sin, t1 * sin + t2 * cos], axis=-1)


def fox_attention(q, k, v, f_logit, b_f):
    B, S, _ = q.shape
    H, d = N_MAIN_HEADS, HEAD_DIM
    q = q.reshape(B, S, H, d).transpose(0, 2, 1, 3)
    k = k.reshape(B, S, H, d).transpose(0, 2, 1, 3)
    v = v.reshape(B, S, H, d).transpose(0, 2, 1, 3)
    log_f = jax.nn.log_sigmoid(f_logit.astype(jnp.float32) + b_f.astype(jnp.float32))
    c = jnp.cumsum(log_f, axis=1).transpose(0, 2, 1)
    nb = S // BLOCK_Q
    qb = q.reshape(B, H, nb, BLOCK_Q, d).transpose(2, 0, 1, 3, 4)
    cb = c.reshape(B, H, nb, BLOCK_Q).transpose(2, 0, 1, 3)
    starts = jnp.arange(nb, dtype=jnp.int32) * BLOCK_Q
    key_pos = jnp.arange(S, dtype=jnp.int32)
    scale = 1.0 / math.sqrt(d)

    def block(args):
        qi, ci, st = args
        qpos = st + jnp.arange(BLOCK_Q, dtype=jnp.int32)
        s = jnp.einsum('bhqd,bhkd->bhqk', qi, k).astype(jnp.float32) * scale
        s = s + ci[..., None] - c[:, :, None, :]
        mask = key_pos[None, :] <= qpos[:, None]
        s = jnp.where(mask[None, None], s, NEG)
        p = jax.nn.softmax(s, axis=-1).astype(v.dtype)
        return jnp.einsum('bhqk,bhkd->bhqd', p, v)

    out = lax.map(block, (qb, cb, starts))
    return out.transpose(1, 0, 3, 2, 4).reshape(B, S, H * d)


def retention(q, k, v):
    B, S, _ = q.shape
    H, dk, dv, C = N_MAIN_HEADS, RET_KEY_DIM, HEAD_DIM, RET_CHUNK
    pos = jnp.arange(S, dtype=jnp.float32)
    qf = rotary(q.reshape(B, S, H, dk).astype(jnp.float32), pos)
    kf = rotary(k.reshape(B, S, H, dk).astype(jnp.float32), pos) * (dk ** -0.5)
    vf = v.reshape(B, S, H, dv).astype(jnp.float32)
    lg = jnp.log1p(-jnp.exp2(-5.0 - jnp.arange(H, dtype=jnp.float32)))
    n = jnp.arange(C, dtype=jnp.float32)
    diff = n[:, None] - n[None, :]
    d_inner = jnp.where(diff[None] >= 0, jnp.exp(lg[:, None, None] * jnp.maximum(diff, 0.0)[None]), 0.0)
    xi = jnp.exp(lg[:, None] * (n[None, :] + 1.0))
    zeta = jnp.exp(lg[:, None] * (C - 1.0 - n[None, :]))
    g_chunk = jnp.exp(lg * C)
    nc = S // C

    def to_chunks(t):
        return t.reshape(B, nc, C, H, t.shape[-1]).transpose(1, 0, 3, 2, 4)

    def step(R, xs):
        qc, kc, vc = xs
        inner = jnp.einsum('bhnd,bhmd->bhnm', qc, kc) * d_inner[None]
        o = jnp.einsum('bhnm,bhme->bhne', inner, vc)
        o = o + jnp.einsum('bhnd,bhde->bhne', qc, R) * xi[None, :, :, None]
        R = g_chunk[None, :, None, None] * R + jnp.einsum('bhmd,bhme->bhde', kc * zeta[None, :, :, None], vc)
        return R, o

    R0 = jnp.zeros((B, H, dk, dv), jnp.float32)
    _, out = lax.scan(step, R0, (to_chunks(qf), to_chunks(kf), to_chunks(vf)))
    out = out.transpose(1, 0, 3, 2, 4).reshape(B, S, H, dv)
    out = out * lax.rsqrt(jnp.mean(out * out, axis=-1, keepdims=True) + EPS)
    return out.reshape(B, S, H * dv).astype(q.dtype)


def mem_cross_attention(q_mem, mem_n, w_kv):
    B, S, _ = q_mem.shape
    kv = mem_n @ w_kv
    km, vm = split_cols(kv, [MEM_W, MEM_W])
    q = q_mem.reshape(B, S, N_MEM_HEADS, HEAD_DIM)
    km = km.reshape(B, -1, N_MEM_HEADS, HEAD_DIM)
    vm = vm.reshape(B, -1, N_MEM_HEADS, HEAD_DIM)
    s = jnp.einsum('bshd,bmhd->bhsm', q, km).astype(jnp.float32) / math.sqrt(HEAD_DIM)
    p = jax.nn.softmax(s, axis=-1).astype(vm.dtype)
    return jnp.einsum('bhsm,bmhd->bshd', p, vm).reshape(B, S, MEM_W)


def setup_inputs(seed: int = 0) -> dict:
    key = jax.random.key(seed)
    ks = jax.random.split(key, 10)
    f32 = jnp.float32
    x = jax.random.normal(ks[0], (BATCH, SEQ, D_MODEL), f32)
    mem = jax.random.normal(ks[1], (BATCH, N_MEM, D_MODEL), f32)
    norm_g = 1.0 + 0.02 * jax.random.normal(ks[2], (DEPTH, D_MODEL), f32)
    fox_w_in = jax.random.normal(ks[3], (N_FOX, D_MODEL, FOX_IN), f32) * D_MODEL ** -0.5
    fox_b_f = 3.0 + 0.5 * jax.random.normal(ks[4], (N_FOX, N_MAIN_HEADS), f32)
    ret_w_in = jax.random.normal(ks[5], (N_RET, D_MODEL, RET_IN), f32) * D_MODEL ** -0.5
    mem_norm_g = 1.0 + 0.02 * jax.random.normal(ks[6], (D_MODEL,), f32)
    w_mem_kv = jax.random.normal(ks[7], (DEPTH, D_MODEL, 2 * MEM_W), f32) * D_MODEL ** -0.5
    w_out = jax.random.normal(ks[8], (DEPTH, INNER, D_MODEL), f32) * INNER ** -0.5
    final_norm_g = 1.0 + 0.02 * jax.random.normal(ks[9], (D_MODEL,), f32)
    return {"x": x, "mem": mem, "norm_g": norm_g, "fox_w_in": fox_w_in, "fox_b_f": fox_b_f,
            "ret_w_in": ret_w_in, "mem_norm_g": mem_norm_g, "w_mem_kv": w_mem_kv,
            "w_out": w_out, "final_norm_g": final_norm_g}


def reference(x, mem, norm_g, fox_w_in, fox_b_f, ret_w_in, mem_norm_g, w_mem_kv, w_out, final_norm_g):
    mem_n = rmsnorm(mem, mem_norm_g)
    for i in range(DEPTH):
        h = rmsnorm(x, norm_g[i])
        j = i // N_MIXERS
        if i % N_MIXERS == 0:
            proj = h @ fox_w_in[j]
            q, k, v, f_logit, q_mem, z = split_cols(
                proj, [MAIN_W, MAIN_W, MAIN_W, N_MAIN_HEADS, MEM_W, INNER])
            main = fox_attention(q, k, v, f_logit, fox_b_f[j])
        else:
            proj = h @ ret_w_in[j]
            qk_w = N_MAIN_HEADS * RET_KEY_DIM
            q, k, v, q_mem, z = split_cols(proj, [qk_w, qk_w, MAIN_W, MEM_W, INNER])
            main = retention(q, k, v)
        memo = mem_cross_attention(q_mem, mem_n, w_mem_kv[i])
        o = jnp.concatenate([main, memo], axis=-1) * jax.nn.silu(z)
        x = x + o @ w_out[i]
    return rmsnorm(x, final_norm_g)
```

```python
import math
import numpy as np
import ml_dtypes
import concourse.bass as bass
import concourse.mybir as mybir
from concourse.bass_utils import run_bass_kernel_spmd

F32 = mybir.dt.float32
BF16 = mybir.dt.bfloat16
AF = mybir.ActivationFunctionType
ALU = mybir.AluOpType

S = 2048
D = 1024
NT = 16
NQ = 4
H = 12
HM = 4
HD = 128
NMEM = 256
INNER = 2048
FOX_IN = 7180
RET_IN = 5632
EPS = 1e-6
SCALE = 1.0 / math.sqrt(HD)
SQ128 = math.sqrt(HD)
MASKNEG = -30000.0

ENGS = ("pe", "act", "dve", "pool", "sp")
N_DMA_SEMS = 40


class Op:
    __slots__ = ("eng", "fn", "deps", "is_dma", "signal", "seq", "semidx", "semval", "prev_dma", "tag")


class Sched:
    def __init__(self):
        self.ops = {e: [] for e in ENGS}
        self.res = {}
        self.n_dma = 0
        self.dma_last = {}
        self.pending_dmas = []

    def add(self, eng, fn, reads=(), writes=(), dma=False, tag=None):
        op = Op()
        op.eng = eng
        op.fn = fn
        op.is_dma = dma
        op.signal = dma
        op.seq = None
        op.tag = tag
        op.prev_dma = None
        deps = {}

        def need(p, kind):
            if p is op:
                return
            if p.is_dma or dma:
                deps[id(p)] = p
            elif p.eng == eng:
                if eng != "pe" and kind == "raw":
                    deps[id(p)] = p
            else:
                deps[id(p)] = p

        for r in reads:
            st = self.res.setdefault(r, [None, []])
            if st[0] is not None:
                need(st[0], "raw")
        for w in writes:
            st = self.res.setdefault(w, [None, []])
            if st[0] is not None:
                need(st[0], "waw")
            for rd in st[1]:
                need(rd, "war")
        for r in reads:
            self.res[r][1].append(op)
        for w in writes:
            st = self.res[w]
            st[0] = op
            st[1] = []
        op.deps = list(deps.values())
        for p in op.deps:
            p.signal = True
        if dma:
            k = self.n_dma % N_DMA_SEMS
            op.semidx = k
            op.semval = 16 * (self.n_dma // N_DMA_SEMS + 1)
            op.prev_dma = self.dma_last.get(k)
            self.dma_last[k] = op
            self.n_dma += 1
            self.pending_dmas.append(op)
        self.ops[eng].append(op)
        return op

    def barrier(self):
        lasts = []
        for e in ENGS:
            for op in reversed(self.ops[e]):
                if not op.is_dma and op.fn is not None:
                    lasts.append(op)
                    break
        dmas = list(self.pending_dmas)
        for e in ENGS:
            op = Op()
            op.eng = e
            op.fn = None
            op.is_dma = False
            op.signal = False
            op.seq = None
            op.tag = "barrier"
            op.prev_dma = None
            op.deps = [p for p in lasts if p.eng != e] + dmas
            for p in op.deps:
                p.signal = True
            self.ops[e].append(op)
        self.res = {}
        self.pending_dmas = []

    def emit(self, nc, stack):
        eng_sems = {e: stack.enter_context(nc.semaphore("s_" + e)) for e in ENGS}
        dma_sems = [stack.enter_context(nc.semaphore("d_%d" % i)) for i in range(N_DMA_SEMS)]
        for e in ENGS:
            c = 0
            for op in self.ops[e]:
                if op.is_dma:
                    continue
                if op.signal:
                    c += 1
                    op.seq = c
        ops = self.ops

        def run(e, engine):
            waited = {}

            def wait_for(p):
                if p.is_dma:
                    key = ("d", p.semidx)
                    val = p.semval
                    sem = dma_sems[p.semidx]
                else:
                    key = ("e", p.eng)
                    val = p.seq
                    sem = eng_sems[p.eng]
                if waited.get(key, 0) < val:
                    engine.wait_ge(sem, val)
                    waited[key] = val

            for op in ops[e]:
                for p in op.deps:
                    wait_for(p)
                if op.prev_dma is not None:
                    wait_for(op.prev_dma)
                if op.fn is None:
                    continue
                inst = op.fn(engine)
                if op.is_dma:
                    inst.then_inc(dma_sems[op.semidx], 16)
                elif op.signal:
                    inst.then_inc(eng_sems[e], 1)

        block = stack.enter_context(nc.Block())

        @block.tensor
        def _(eng):
            run("pe", eng)

        @block.scalar
        def _(eng):
            run("act", eng)

        @block.vector
        def _(eng):
            run("dve", eng)

        @block.gpsimd
        def _(eng):
            run("pool", eng)

        @block.sync
        def _(eng):
            run("sp", eng)


def _bf(a):
    return np.asarray(a, dtype=np.float32).astype(ml_dtypes.bfloat16)


_CONSTS = None


def host_consts():
    global _CONSTS
    if _CONSTS is not None:
        return _CONSTS
    c = {}
    c["ident_bf"] = _bf(np.eye(128))
    c["ident_f"] = np.eye(128, dtype=np.float32)
    i = np.arange(128)
    c["tri_f"] = (i[:, None] <= i[None, :]).astype(np.float32)
    c["ones_f"] = np.ones((128, 128), np.float32)
    c["ones2_bf"] = _bf(np.full((128, 128), 2.0))
    c["onesg_bf"] = _bf(np.full((128, 128), 4.0 / 128.0))
    c["maskneg"] = _bf(np.where(i[:, None] <= i[None, :], 0.0, MASKNEG))
    sel = np.zeros((76, H, 128), np.float32)
    for h in range(H):
        sel[h, h, :] = 1.0
        sel[32 + h, h, :] = 1.0
        sel[64 + h, h, :] = 1.0
    c["sel"] = _bf(sel.reshape(76, H * 128))
    hh = np.arange(H, dtype=np.float64)
    lg = np.log1p(-np.exp2(-5.0 - hh))
    n = np.arange(128, dtype=np.float64)
    diff = n[None, :] - n[:, None]
    dm = np.where(diff[:, None, :] >= 0, np.exp(-lg[None, :, None] * (n[:, None, None] + 1.0)), 0.0)
    c["dmask"] = (dm * 0.125).astype(np.float32).reshape(128, H * 128)
    c["zeta"] = (0.125 * np.exp(lg[None, :] * (127.0 - n[:, None]))).astype(np.float32)
    c["xi"] = np.exp(lg[None, :] * (n[:, None] + 1.0)).astype(np.float32)
    c["gchunk"] = [float(np.exp(lg[h] * 128.0)) for h in range(H)]
    half = 32
    inv = 1.0 / (10000.0 ** (np.arange(half, dtype=np.float64) / half))
    pos = np.arange(S, dtype=np.float64)
    ang = pos[:, None] * inv[None, :]
    cos = np.cos(ang).reshape(NT, 128, half).transpose(1, 0, 2)
    sin = np.sin(ang).reshape(NT, 128, half).transpose(1, 0, 2)
    c["cc"] = np.concatenate([cos, cos], axis=-1).astype(np.float32).reshape(128, NT * 64)
    c["ss"] = np.concatenate([-sin, sin], axis=-1).astype(np.float32).reshape(128, NT * 64)
    _CONSTS = c
    return c


CONST_SPECS = [
    ("ident_bf", [128, 128], BF16), ("ident_f", [128, 128], F32), ("tri_f", [128, 128], F32),
    ("ones_f", [128, 128], F32), ("ones2_bf", [128, 128], BF16), ("onesg_bf", [128, 128], BF16),
    ("maskneg", [128, 128], BF16), ("sel", [76, H * 128], BF16),
    ("dmask", [128, H * 128], F32), ("zeta", [128, H], F32), ("xi", [128, H], F32),
    ("cc", [128, NT * 64], F32), ("ss", [128, NT * 64], F32),
]


class Builder:
    def __init__(self, nc, stack, n_layers=2, dbg=None):
        self.dbg = dbg
        self.nc = nc
        self.stack = stack
        self.sc = Sched()
        self.n_layers = n_layers
        self.evac_rr = 0
        self.ps_rr = 0
        self.scr_rr = 0

    def sb(self, name, shape, dt):
        return self.stack.enter_context(self.nc.sbuf_tensor(name, shape, dt))

    def dram_in(self, name, shape, dt):
        return self.nc.dram_tensor(name, shape, dt, kind="ExternalInput")

    def arena_reset(self):
        self.arena_off = 0

    def arena(self, nelem, dt):
        nbytes = nelem * (4 if dt == F32 else 2)
        nbytes = (nbytes + 63) // 64 * 64
        off = self.arena_off
        self.arena_off += nbytes
        assert self.arena_off <= self.ARENA_BYTES, (self.arena_off, self.ARENA_BYTES)
        if dt == F32:
            return self.arena_f[:, off // 4: off // 4 + nelem]
        return self.arena_b[:, off // 2: off // 2 + nelem]

    def dma(self, q, out, in_, reads=(), writes=()):
        return self.sc.add(q, lambda e: e.dma_start(out=out, in_=in_), reads, writes, dma=True)

    def mm(self, out, lhsT, rhs, start, stop, reads, writes):
        return self.sc.add("pe", lambda e: e.matmul(out, lhsT, rhs, start=start, stop=stop), reads, writes)

    def tr(self, out, in_, ident, reads, writes):
        return self.sc.add("pe", lambda e: e.transpose(out, in_, ident), reads, writes)

    def act(self, out, in_, func, reads, writes, bias=None, scale=None, accum_out=None):
        kw = {}
        if bias is not None:
            kw["bias"] = bias
        if scale is not None:
            kw["scale"] = scale
        if accum_out is not None:
            kw["accum_out"] = accum_out
        return self.sc.add("act", lambda e: e.activation(out, in_, func, **kw), reads, writes)

    def v(self, eng, name, *args, reads, writes):
        return self.sc.add(eng, lambda e: getattr(e, name)(*args), reads, writes)

    def copy(self, eng, out, in_, reads, writes):
        if eng == "act":
            return self.sc.add("act", lambda e: e.copy(out, in_), reads, writes)
        return self.sc.add(eng, lambda e: e.tensor_copy(out, in_), reads, writes)

    def evac_eng(self):
        self.evac_rr += 1
        return "dve" if (self.evac_rr % 2) else "act"

    def pbank(self):
        b = self.ps_rr % 2
        self.ps_rr += 1
        return b

    def build(self):
        nc = self.nc
        x_d = self.dram_in("x", [S, D], F32)
        mem_d = self.dram_in("mem", [NMEM, D], F32)
        gb_d = self.dram_in("gb", [4, 128, D], F32)
        bfb_d = self.dram_in("bfb", [128, NT * H], F32)
        fox_w = self.dram_in("fox_w_in", [D, FOX_IN], F32)
        ret_w = self.dram_in("ret_w_in", [D, RET_IN], F32)
        wkv_d = self.dram_in("w_mem_kv", [2, D, 2 * HM * HD], F32)
        wout_d = self.dram_in("w_out", [2, INNER, D], F32)
        self.cd = {}
        for name, shape, dt in CONST_SPECS:
            self.cd[name] = self.dram_in(name, shape, dt)
        self.out_d = nc.dram_tensor("out", [S, D], F32, kind="ExternalOutput")
        self.bfb_d = bfb_d
        if self.dbg:
            self.dbg1 = nc.dram_tensor("dbg1", [128, 4 * S], BF16, kind="ExternalOutput")
            self.dbg2 = nc.dram_tensor("dbg2", [128, 4 * S], BF16, kind="ExternalOutput")

        sb = self.sb
        self.xres = sb("xres", [128, NT, D], F32)
        self.hT = sb("hT", [128, 8, S], BF16)
        self.oT = sb("oT", [128, 4, S], BF16)
        self.Vg = sb("Vg", [128, NT, 512], BF16)
        self.zT = sb("zT", [128, S], BF16)
        self.wA = [sb("wA%d" % i, [128, 8, 128], BF16) for i in range(3)]
        self.wV = sb("wV", [128, 8, 512], BF16)
        self.wO = sb("wO", [128, 4, D], BF16)
        self.PT = [sb("PT%d" % i, [128, 512], BF16) for i in range(3)]
        self.f32ab = sb("f32ab", [128, D], F32)
        self.f32a = self.f32ab[:, 0:512]
        self.f32b = self.f32ab[:, 512:1024]
        self.scr = [sb("scr%d" % i, [128, D], BF16) for i in range(2)]
        self.ss_t = sb("ss_t", [128, NT], F32)
        self.rstd = sb("rstd", [128, NT], F32)
        self.epsb = sb("epsb", [128, 2], F32)
        self.memT = sb("memT", [128, 8, NMEM], BF16)
        self.KmT = sb("KmT", [128, HM, NMEM], BF16)
        self.Vm = sb("Vm", [128, 2, HM * HD], BF16)
        self.cst = {}
        for name in ("ident_bf", "ident_f", "ones2_bf", "onesg_bf"):
            spec = [s for s in CONST_SPECS if s[0] == name][0]
            self.cst[name] = sb("c_" + name, spec[1], spec[2])
        self.ARENA_BYTES = 33 * 1024
        self.arena_b = sb("arena", [128, self.ARENA_BYTES // 2], BF16)
        self.arena_f = self.arena_b.bitcast(F32)
        self.ps = [self.stack.enter_context(nc.psum_tensor("ps%d" % i, [128, 512], F32)) for i in range(8)]

        for name in ("ident_bf", "ident_f", "ones2_bf", "onesg_bf"):
            self.dma("sp", self.cst[name][:], self.cd[name][:], writes=[("c", name)])

        self.v("dve", "memset", self.epsb[:, 0:1], EPS, reads=[], writes=[("epsb",)])
        self.v("dve", "memset", self.epsb[:, 1:2], 4.0 * EPS, reads=[], writes=[("epsb",)])
        xv = x_d.rearrange("(t p) d -> p t d", p=128)
        for a in range(4):
            self.dma("sp", self.xres[:, 4 * a:4 * a + 4, :], xv[:, 4 * a:4 * a + 4, :],
                     writes=[("x", t) for t in range(4 * a, 4 * a + 4)])

        self.mem_norm(mem_d, gb_d)

        self.rms_to_hT(0, gb_d)
        self.fox_layer(fox_w, wkv_d, wout_d)
        if self.n_layers > 1:
            self.sc.barrier()
            self.rms_to_hT(1, gb_d)
            self.ret_layer(ret_w, wkv_d, wout_d)
        self.final_norm(gb_d)
        self.sc.barrier()
        self.sc.emit(nc, self.stack)

    def rms_rows(self, src_fn, nblk, g_idx, gb_d, dst_fn, src_res):
        gbt = self.f32ab
        self.dma("sp", gbt[:], gb_d[g_idx], writes=[("f32a",), ("f32b",)])
        junk = self.zT
        pb = self.ps[7].bitcast(BF16)
        for t in range(nblk):
            src = src_fn(t)
            scr = self.scr[self.scr_rr % 2]
            sres = ("scr", self.scr_rr % 2)
            self.scr_rr += 1
            self.act(junk[:, 0:D], src, AF.Square, reads=[src_res(t)], writes=[("zT",), ("ss", t)],
                     accum_out=self.ss_t[:, t:t + 1])
            self.act(self.rstd[:, t:t + 1], self.ss_t[:, t:t + 1], AF.Sqrt, reads=[("ss", t)], writes=[("rstd", t)],
                     scale=1.0 / D, bias=self.epsb[:, 0:1])
            self.v("dve", "reciprocal", self.rstd[:, t:t + 1], self.rstd[:, t:t + 1],
                   reads=[("rstd", t)], writes=[("rstd", t)])
            self.v("dve", "scalar_tensor_tensor",
                scr[:], src, self.rstd[:, t:t + 1], gbt[:], ALU.mult, ALU.mult,
                reads=[src_res(t), ("rstd", t), ("f32a",), ("f32b",)], writes=[sres])
            for c in range(8):
                self.tr(pb[:, c * 128:(c + 1) * 128], scr[:, c * 128:(c + 1) * 128],
                        self.cst["ident_bf"][:], reads=[sres, ("c", "ident_bf")], writes=[("ps", 7)])
            dst, dres = dst_fn(t)
            self.copy("act", dst, pb[:, 0:1024].rearrange("p (c k) -> p c k", c=8),
                      reads=[("ps", 7)], writes=[dres])

    def mem_norm(self, mem_d, gb_d):
        memv = mem_d.rearrange("(t p) d -> p t d", p=128)
        memraw = self.oT.bitcast(F32)[:, 0:2, :]
        ores = [("oT", c) for c in range(4)]
        self.dma("sp", memraw, memv, writes=ores)
        self.rms_rows(lambda t: memraw[:, t, :], 2, 3, gb_d,
                      lambda t: (self.memT[:, :, t * 128:(t + 1) * 128], ("memT",)),
                      lambda t: ores[0])

    def rms_to_hT(self, li, gb_d):
        self.rms_rows(lambda t: self.xres[:, t, :], NT, li, gb_d,
                      lambda t: (self.hT[:, :, t * 128:(t + 1) * 128], ("hT", t // 4)),
                      lambda t: ("x", t))

    def load_w(self, dst, w_d, col0, ncol, res):
        src = w_d.rearrange("(c p) n -> p c n", p=128)[:, :, col0:col0 + ncol]
        return self.dma("pool", dst, src, writes=[res] if not isinstance(res, list) else res)

    def proj_feat(self, wt, wres, post):
        for tb in range(NQ):
            bank = self.pbank()
            for c in range(8):
                self.mm(self.ps[bank][:, :], wt[:, c, :], self.hT[:, c, tb * 512:(tb + 1) * 512],
                        start=(c == 0), stop=(c == 7), reads=[wres, ("hT", tb)], writes=[("ps", bank)])
            post(tb, bank)

    def proj_tok(self, wt, wres, ncol, post):
        for t in range(NT):
            bank = self.pbank()
            for c in range(8):
                self.mm(self.ps[bank][:, 0:ncol], self.hT[:, c, t * 128:(t + 1) * 128], wt[:, c, 0:ncol],
                        start=(c == 0), stop=(c == 7), reads=[wres, ("hT", t // 4)], writes=[("ps", bank)])
            post(t, bank)

    def evac_bf16(self, dst, dres):
        def post(tb, bank):
            self.copy(self.evac_eng(), dst[:, tb * 512:(tb + 1) * 512], self.ps[bank][:, :],
                      reads=[("ps", bank)], writes=dres if isinstance(dres, list) else [dres])
        return post

    def evac_gate(self):
        def post(tb, bank):
            self.act(self.f32a, self.ps[bank][:, :], AF.Tanh, reads=[("ps", bank)], writes=[("f32a",)], scale=0.5)
            self.v("dve", "scalar_tensor_tensor", self.zT[:, tb * 512:(tb + 1) * 512], self.f32a, 1.0,
                                                           self.ps[bank][:, :], ALU.add, ALU.mult,
                   reads=[("f32a",), ("ps", bank)], writes=[("zT",)])
        return post

    def v_group(self, w_d, col0):
        self.load_w(self.wV[:], w_d, col0, 512, ("wV",))

        def post(t, bank):
            self.copy(self.evac_eng(), self.Vg[:, t, :], self.ps[bank][:, :],
                      reads=[("ps", bank)], writes=[("Vg", t)])
        self.proj_tok(self.wV, ("wV",), 512, post)

    def softmax_attn_tile(self, kt_ap, kt_res, q_sl, q_res, v_ap, v_res, col0, first, last,
                          bias_ap, bias_res, slot, extra=None):
        sbank = 2 + slot
        w = slice(col0, 512)
        qr = q_res if isinstance(q_res, list) else [q_res]
        self.mm(self.ps[sbank][:, w], kt_ap, q_sl, start=True, stop=(extra is None),
                reads=[kt_res] + qr, writes=[("ps", sbank)])
        if extra is not None:
            extra(sbank)
        kw = dict(scale=SCALE)
        if bias_ap is not None:
            kw["bias"] = bias_ap
        rd = [("ps", sbank)] + ([bias_res] if bias_res is not None else [])
        self.act(self.PT[slot][:, w], self.ps[sbank][:, w], AF.Exp, reads=rd, writes=[("PT", slot)], **kw)

        def pv():
            self.mm(self.ps[5][:, w], v_ap, self.PT[slot][:, w], start=first, stop=last,
                    reads=[v_res, ("PT", slot)], writes=[("ps", 5)])
            self.mm(self.ps[6][:, w], self.cst["ones2_bf"][:], self.PT[slot][:, w], start=first, stop=last,
                    reads=[("c", "ones2_bf"), ("PT", slot)], writes=[("ps", 6)])
        return pv

    def attn_finish(self, I, hh):
        sl = slice(I * 512, (I + 1) * 512)
        self.v("dve", "reciprocal", self.f32b, self.ps[6][:, :], reads=[("ps", 6)], writes=[("f32b",)])
        self.v("dve", "tensor_tensor", self.f32b, self.ps[5][:, :], self.f32b, ALU.mult,
               reads=[("ps", 5), ("f32b",)], writes=[("f32b",)])
        self.v("dve", "tensor_tensor", self.oT[:, hh, sl], self.f32b, self.zT[:, sl], ALU.mult,
               reads=[("f32b",), ("zT",)], writes=[("oT", hh)])

    def run_pipelined(self, tiles):
        pend = []
        for i, t in enumerate(tiles):
            pend.append(t(i % 3))
            if len(pend) > 2:
                pend.pop(0)()
        for p in pend:
            p()

    def out_proj(self, li, g, wout_d):
        src = wout_d[li, g * 512:(g + 1) * 512, :].rearrange("(c p) n -> p c n", p=128)
        self.dma("pool", self.wO[:], src, writes=[("wO",)])
        for t in range(NT):
            for half in range(2):
                bank = self.pbank()
                for c in range(4):
                    self.mm(self.ps[bank][:, :], self.oT[:, c, t * 128:(t + 1) * 128],
                            self.wO[:, c, half * 512:(half + 1) * 512], start=(c == 0), stop=(c == 3),
                            reads=[("oT", c), ("wO",)], writes=[("ps", bank)])
                xs = self.xres[:, t, half * 512:(half + 1) * 512]
                self.v("dve", "tensor_tensor", xs, xs, self.ps[bank][:, :], ALU.add,
                       reads=[("x", t), ("ps", bank)], writes=[("x", t)])

    def mem_kv(self, li, wkv_d):
        wrows = wkv_d[li]
        self.load_w(self.wV[:], wrows, 0, 512, ("wV",))
        for hm in range(HM):
            bank = self.pbank()
            for c in range(8):
                self.mm(self.ps[bank][:, 0:NMEM], self.wV[:, c, hm * 128:(hm + 1) * 128], self.memT[:, c, :],
                        start=(c == 0), stop=(c == 7), reads=[("wV",), ("memT",)], writes=[("ps", bank)])
            self.copy(self.evac_eng(), self.KmT[:, hm, :], self.ps[bank][:, 0:NMEM],
                      reads=[("ps", bank)], writes=[("KmT",)])
        self.load_w(self.wV[:], wrows, 512, 512, ("wV",))
        for mb in range(2):
            bank = self.pbank()
            for c in range(8):
                self.mm(self.ps[bank][:, :], self.memT[:, c, mb * 128:(mb + 1) * 128], self.wV[:, c, :],
                        start=(c == 0), stop=(c == 7), reads=[("wV",), ("memT",)], writes=[("ps", bank)])
            self.copy(self.evac_eng(), self.Vm[:, mb, :], self.ps[bank][:, :],
                      reads=[("ps", bank)], writes=[("Vm",)])

    def mem_heads(self, li, w_d, qm_col0, z_col0, wkv_d, wout_d, QT, qres):
        self.mem_kv(li, wkv_d)
        for hm in range(HM):
            wq, wz = self.wA[0], self.wA[2]
            self.load_w(wq[:], w_d, qm_col0 + hm * 128, 128, ("wA", 0))
            self.load_w(wz[:], w_d, z_col0 + (H + hm) * 128, 128, ("wA", 2))
            self.proj_feat(wq, ("wA", 0), self.evac_bf16(QT, qres))
            self.proj_feat(wz, ("wA", 2), self.evac_gate())
            for I in range(NQ):
                tiles = []
                for mb in range(2):
                    def mk(slot, mb=mb, I=I, hm=hm):
                        return self.softmax_attn_tile(
                            self.KmT[:, hm, mb * 128:(mb + 1) * 128], ("KmT",),
                            QT[:, I * 512:(I + 1) * 512], qres,
                            self.Vm[:, mb, hm * 128:(hm + 1) * 128], ("Vm",),
                            0, mb == 0, mb == 1, None, None, slot)
                    tiles.append(mk)
                self.run_pipelined(tiles)
                self.attn_finish(I, hm)
        self.out_proj(li, 3, wout_d)

    def fox_gate(self, fox_w):
        NH = NT * H
        self.load_w(self.wf, fox_w, 3 * H * HD, H, ("wf",))
        self.dma("sp", self.bfb, self.bfb_d[:], writes=[("bfb",)])
        for name in ("tri_f", "ones_f", "maskneg", "sel"):
            self.dma("sp", self.cst[name], self.cd[name][:], writes=[("c", name)])
        bank = 0
        for t in range(NT):
            for c in range(8):
                self.mm(self.ps[bank][:, t * H:(t + 1) * H], self.hT[:, c, t * 128:(t + 1) * 128], self.wf[:, c, :],
                        start=(c == 0), stop=(c == 7), reads=[("wf",), ("hT", t // 4)], writes=[("ps", bank)])
        nl, cn, off = self.nl, self.cn, self.off
        self.v("dve", "tensor_tensor", nl, self.ps[0][:, 0:NH], self.bfb, ALU.add,
               reads=[("ps", 0), ("bfb",)], writes=[("nl",)])
        self.act(nl, nl, AF.Exp, reads=[("nl",)], writes=[("nl",)], scale=-1.0)
        self.act(nl, nl, AF.Ln, reads=[("nl",)], writes=[("nl",)], bias=1.0)
        self.mm(self.ps[1][:, 0:NH], self.cst["tri_f"], nl, start=True, stop=True,
                reads=[("c", "tri_f"), ("nl",)], writes=[("ps", 1)])
        self.mm(self.ps[0][:, 0:NH], self.cst["ones_f"], nl, start=True, stop=True,
                reads=[("c", "ones_f"), ("nl",)], writes=[("ps", 0)])
        self.v("dve", "memset", off[:, 0:H], 0.0, reads=[], writes=[("off",)])
        for t in range(1, NT):
            self.v("dve", "tensor_tensor", off[:, t * H:(t + 1) * H], off[:, (t - 1) * H:t * H],
                                                         self.ps[0][:, (t - 1) * H:t * H], ALU.add,
                   reads=[("off",), ("ps", 0)], writes=[("off",)])
        self.v("dve", "tensor_tensor", cn, self.ps[1][:, 0:NH], off, ALU.add,
               reads=[("ps", 1), ("off",)], writes=[("cn",)])
        csp = self.csp
        csp3 = csp.rearrange("p (t k) -> p t k", k=76)
        vv, r1, hb = self.sp_v, self.sp_r, self.sp_hb
        self.v("dve", "memset", csp, 0.0, reads=[], writes=[("csp",)])
        self.v("dve", "tensor_scalar", vv, cn, -SQ128, None, ALU.mult, reads=[("cn",)], writes=[("spv",)])
        cur = vv
        for k, p0 in enumerate((0, 32, 64)):
            self.v("dve", "tensor_copy", hb, cur, reads=[("spv",), ("spr",)], writes=[("sphb",)])
            self.v("dve", "tensor_copy", csp3[:, :, p0:p0 + H], hb.rearrange("p (t h) -> p t h", h=H),
                   reads=[("sphb",)], writes=[("csp",)])
            if k < 2:
                nxt = r1 if cur is vv else vv
                self.v("dve", "tensor_tensor", nxt, cur, hb, ALU.subtract,
                       reads=[("spv",), ("spr",), ("sphb",)], writes=[("spv",), ("spr",)])
                cur = nxt
        for a in range(4):
            for tt in range(4):
                t = 4 * a + tt
                self.tr(self.ps[0][0:76, tt * 128:(tt + 1) * 128], csp3[:, t, :],
                        self.cst["ident_f"][:], reads=[("csp",), ("c", "ident_f")], writes=[("ps", 0)])
            self.v("dve", "tensor_copy", self.csplit[0:76, a * 512:(a + 1) * 512], self.ps[0][0:76, :],
                   reads=[("ps", 0)], writes=[("csplit",)])

    def fox_layer(self, fox_w, wkv_d, wout_d):
        QOFF, KOFF, VOFF, QMOFF, ZOFF = 0, H * HD, 2 * H * HD, 3 * H * HD + H, 3 * H * HD + H + HM * HD
        NH = NT * H
        self.arena_reset()
        QT = self.arena(S, BF16)
        KT = self.arena(S, BF16)
        self.csplit = self.arena(S, BF16)
        self.cst["sel"] = self.arena(H * 128, BF16)[0:76, :]
        self.cst["maskneg"] = self.arena(128, BF16)
        self.cst["tri_f"] = self.arena(128, F32)
        self.cst["ones_f"] = self.arena(128, F32)
        self.csp = self.arena(NT * 76, F32)
        self.sp_v = self.arena(NH, F32)
        self.sp_r = self.arena(NH, F32)
        self.sp_hb = self.arena(NH, BF16)
        self.bfb = self.arena(NH, F32)
        self.nl = self.arena(NH, F32)
        self.cn = self.arena(NH, F32)
        self.off = self.arena(NH, F32)
        self.wf = self.arena(8 * H, BF16).rearrange("p (c h) -> p c h", h=H)

        self.fox_gate(fox_w)
        for g in range(3):
            self.v_group(fox_w, VOFF + g * 512)
            for hh in range(4):
                h = 4 * g + hh
                wq, wk, wz = self.wA
                self.load_w(wq[:], fox_w, QOFF + h * 128, 128, ("wA", 0))
                self.load_w(wk[:], fox_w, KOFF + h * 128, 128, ("wA", 1))
                self.load_w(wz[:], fox_w, ZOFF + h * 128, 128, ("wA", 2))
                self.proj_feat(wq, ("wA", 0), self.evac_bf16(QT, ("QT",)))
                self.proj_feat(wk, ("wA", 1), self.evac_bf16(KT, ("KT",)))
                self.proj_feat(wz, ("wA", 2), self.evac_gate())
                for I in range(NQ):
                    tiles = []
                    nj = 4 * I + 4
                    for j in range(nj):
                        m = j - 4 * I
                        col0 = 128 * m if m >= 0 else 0

                        def mk(slot, j=j, m=m, col0=col0, I=I, nj=nj, h=h, hh=hh):
                            def extra(sbank):
                                if m >= 0:
                                    self.mm(self.ps[sbank][:, col0:col0 + 128], self.cst["ident_bf"][:],
                                            self.cst["maskneg"], start=False, stop=False,
                                            reads=[("c", "ident_bf"), ("c", "maskneg")], writes=[("ps", sbank)])
                                self.mm(self.ps[sbank][:, col0:512], self.cst["sel"][:, h * 128:(h + 1) * 128],
                                        self.csplit[0:76, I * 512 + col0:(I + 1) * 512], start=False, stop=True,
                                        reads=[("c", "sel"), ("csplit",)], writes=[("ps", sbank)])
                            return self.softmax_attn_tile(
                                KT[:, j * 128:(j + 1) * 128], ("KT",),
                                QT[:, I * 512 + col0:(I + 1) * 512], ("QT",),
                                self.Vg[:, j, hh * 128:(hh + 1) * 128], ("Vg", j),
                                col0, j == 0, j == nj - 1,
                                self.cn[:, j * H + h:j * H + h + 1], ("cn",), slot, extra=extra)
                        tiles.append(mk)
                    self.run_pipelined(tiles)
                    self.attn_finish(I, hh)
            self.out_proj(0, g, wout_d)
        self.mem_heads(0, fox_w, QMOFF, ZOFF, wkv_d, wout_d, QT, ("QT",))

    def ret_layer(self, ret_w, wkv_d, wout_d):
        QOFF, KOFF, VOFF, QMOFF, ZOFF = 0, H * 64, 2 * H * 64, 2 * H * 64 + H * HD, 2 * H * 64 + H * HD + HM * HD
        gch = host_consts()["gchunk"]
        self.arena_reset()
        QxT = self.arena(S, BF16)
        KT2 = self.arena(S, BF16)
        Kz = self.arena(NT * 128, BF16).rearrange("p (t k) -> p t k", k=128)
        Rb = self.arena(NT * 128, BF16).rearrange("p (t k) -> p t k", k=128)
        Rf = self.arena(128, F32)
        cc = self.arena(NT * 64, F32).rearrange("p (t k) -> p t k", k=64)
        ss = self.arena(NT * 64, F32).rearrange("p (t k) -> p t k", k=64)
        zeta = self.arena(H, F32)
        xi = self.arena(H, F32)
        dmk = [self.arena(128, F32) for _ in range(2)]
        ra = self.arena(256, F32)
        rb = self.arena(256, F32)
        qkb = self.arena(256, BF16)
        wqk = self.arena(8 * 256, BF16).rearrange("p (c k) -> p c k", k=256)
        self.dma("sp", cc, self.cd["cc"].rearrange("p (t k) -> p t k", k=64), writes=[("cc",)])
        self.dma("sp", ss, self.cd["ss"].rearrange("p (t k) -> p t k", k=64), writes=[("ss",)])
        self.dma("sp", zeta, self.cd["zeta"][:], writes=[("zeta",)])
        self.dma("sp", xi, self.cd["xi"][:], writes=[("xi",)])
        pb7 = self.ps[7].bitcast(BF16)
        ra4 = ra.rearrange("p (a k) -> p a k", k=64)
        rb4 = rb.rearrange("p (a k) -> p a k", k=64)
        wsrc = ret_w.rearrange("(c p) n -> p c n", p=128)
        qx_all = [("QxT", a) for a in range(4)]

        for pr in range(H // 2):
            h0 = 2 * pr
            if pr % 2 == 0:
                self.v_group(ret_w, VOFF + (pr // 2) * 512)
            self.dma("pool", wqk[:, :, 0:128], wsrc[:, :, QOFF + h0 * 64:QOFF + h0 * 64 + 128], writes=[("wqk",)])
            self.dma("pool", wqk[:, :, 128:256], wsrc[:, :, KOFF + h0 * 64:KOFF + h0 * 64 + 128], writes=[("wqk",)])

            def post(t, bank, h0=h0):
                p4 = self.ps[bank][:, 0:256].rearrange("p (a k) -> p a k", k=64)
                self.v("dve", "tensor_tensor", ra4, p4, cc[:, t:t + 1, :].broadcast_to([128, 4, 64]), ALU.mult,
                       reads=[("ps", bank), ("cc",)], writes=[("ra",)])
                self.v("dve", "tensor_tensor", rb4[:, :, 0:32], p4[:, :, 32:64],
                       ss[:, t:t + 1, 0:32].broadcast_to([128, 4, 32]), ALU.mult,
                       reads=[("ps", bank), ("ss",)], writes=[("rb",)])
                self.v("dve", "tensor_tensor", rb4[:, :, 32:64], p4[:, :, 0:32],
                       ss[:, t:t + 1, 32:64].broadcast_to([128, 4, 32]), ALU.mult,
                       reads=[("ps", bank), ("ss",)], writes=[("rb",)])
                self.v("dve", "tensor_tensor", ra, ra, rb, ALU.add, reads=[("ra",), ("rb",)], writes=[("ra",)])
                self.v("dve", "tensor_tensor", qkb[:, 0:128].rearrange("p (a k) -> p a k", k=64), ra4[:, 0:2, :],
                       xi[:, h0:h0 + 2].unsqueeze(2).to_broadcast([128, 2, 64]), ALU.mult,
                       reads=[("ra",), ("xi",)], writes=[("qkb",)])
                self.v("dve", "tensor_copy", qkb[:, 128:256], ra[:, 128:256], reads=[("ra",)], writes=[("qkb",)])
                self.v("dve", "tensor_tensor", Kz[:, t, :].rearrange("p (a k) -> p a k", k=64), ra4[:, 2:4, :],
                       zeta[:, h0:h0 + 2].unsqueeze(2).to_broadcast([128, 2, 64]), ALU.mult,
                       reads=[("ra",), ("zeta",)], writes=[("Kz", t)])
                tt = t % 4
                self.tr(pb7[:, (2 * tt) * 128:(2 * tt + 1) * 128], qkb[:, 0:128], self.cst["ident_bf"][:],
                        reads=[("qkb",), ("c", "ident_bf")], writes=[("ps", 7)])
                self.tr(pb7[:, (2 * tt + 1) * 128:(2 * tt + 2) * 128], qkb[:, 128:256], self.cst["ident_bf"][:],
                        reads=[("qkb",), ("c", "ident_bf")], writes=[("ps", 7)])
                if tt == 3:
                    a = t // 4
                    pv4 = pb7[:, 0:1024].rearrange("p (t w k) -> p t w k", w=2, k=128)
                    self.copy("act", QxT[:, a * 512:(a + 1) * 512].rearrange("p (t k) -> p t k", k=128), pv4[:, :, 0, :],
                              reads=[("ps", 7)], writes=[("QxT", a)])
                    self.copy("act", KT2[:, a * 512:(a + 1) * 512].rearrange("p (t k) -> p t k", k=128), pv4[:, :, 1, :],
                              reads=[("ps", 7)], writes=[("KT2", a)])
            self.proj_tok(wqk, ("wqk",), 256, post)

            for hh2 in range(2):
                h = h0 + hh2
                hv = h % 4
                rows = slice(64 * hh2, 64 * hh2 + 64)
                for a in range(4):
                    bank = self.pbank()
                    ns = [n for n in range(4 * a, 4 * a + 4) if n < NT - 1]
                    for n in ns:
                        self.mm(self.ps[bank][:, (n % 4) * 128:(n % 4 + 1) * 128], Kz[:, n, :],
                                self.Vg[:, n, hv * 128:(hv + 1) * 128], start=True, stop=True,
                                reads=[("Kz", n), ("Vg", n)], writes=[("ps", bank)])
                    for n in ns:
                        u = self.ps[bank][rows, (n % 4) * 128:(n % 4 + 1) * 128]
                        if n == 0:
                            self.v("dve", "tensor_copy", Rf[rows, :], u, reads=[("ps", bank)], writes=[("Rf", hh2)])
                        else:
                            self.v("dve", "scalar_tensor_tensor", Rf[rows, :], Rf[rows, :], gch[h], u,
                                   ALU.mult, ALU.add, reads=[("ps", bank), ("Rf", hh2)], writes=[("Rf", hh2)])
                        self.v("dve", "tensor_copy", Rb[rows, n + 1, :], Rf[rows, :],
                               reads=[("Rf", hh2)], writes=[("Rb", hh2, n + 1)])

            for hh2 in range(2):
                h = h0 + hh2
                hv = h % 4
                rows = slice(64 * hh2, 64 * hh2 + 64)
                self.load_w(self.wA[2][:], ret_w, ZOFF + h * 128, 128, ("wA", 2))
                self.proj_feat(self.wA[2], ("wA", 2), self.evac_gate())
                dm = dmk[h % 2]
                self.dma("sp", dm, self.cd["dmask"][:, h * 128:(h + 1) * 128], writes=[("dmk", h % 2)])
                for I in range(NQ):
                    slot = I % 3
                    sbank = 2 + slot
                    for nn in range(4):
                        n = 4 * I + nn
                        self.mm(self.ps[sbank][:, nn * 128:(nn + 1) * 128], KT2[rows, n * 128:(n + 1) * 128],
                                QxT[rows, n * 128:(n + 1) * 128], start=True, stop=True,
                                reads=[("KT2", I), ("QxT", I)], writes=[("ps", sbank)])
                    Am = self.PT[slot]
                    self.v("dve", "tensor_tensor", Am.rearrange("p (a k) -> p a k", k=128),
                           self.ps[sbank][:, :].rearrange("p (a k) -> p a k", k=128),
                           dm.unsqueeze(1).to_broadcast([128, 4, 128]), ALU.mult,
                           reads=[("ps", sbank), ("dmk", h % 2)], writes=[("PT", slot)])
                    for nn in range(4):
                        n = 4 * I + nn
                        self.mm(self.ps[5][:, nn * 128:(nn + 1) * 128], self.Vg[:, n, hv * 128:(hv + 1) * 128],
                                Am[:, nn * 128:(nn + 1) * 128], start=True, stop=(n == 0),
                                reads=[("Vg", n), ("PT", slot)], writes=[("ps", 5)])
                        if n > 0:
                            self.mm(self.ps[5][:, nn * 128:(nn + 1) * 128], Rb[rows, n, :],
                                    QxT[rows, n * 128:(n + 1) * 128], start=False, stop=True,
                                    reads=[("Rb", hh2, n), ("QxT", I)], writes=[("ps", 5)])
                    s2 = (slot + 1) % 3
                    sq = self.PT[s2]
                    self.act(sq[:, :], self.ps[5][:, :], AF.Square, reads=[("ps", 5)], writes=[("PT", s2)])
                    self.copy("act", self.f32a, self.ps[5][:, :], reads=[("ps", 5)], writes=[("f32a",)])
                    self.mm(self.ps[6][:, :], self.cst["onesg_bf"][:], sq[:, :], start=True, stop=True,
                            reads=[("c", "onesg_bf"), ("PT", s2)], writes=[("ps", 6)])
                    self.act(self.f32b, self.ps[6][:, :], AF.Sqrt, reads=[("ps", 6)], writes=[("f32b",)],
                             bias=self.epsb[:, 1:2])
                    self.v("dve", "reciprocal", self.f32b, self.f32b, reads=[("f32b",)], writes=[("f32b",)])
                    self.v("dve", "tensor_tensor", self.f32b, self.f32b, self.f32a, ALU.mult,
                           reads=[("f32a",), ("f32b",)], writes=[("f32b",)])
                    sl = slice(I * 512, (I + 1) * 512)
                    self.v("dve", "tensor_tensor", self.oT[:, hv, sl], self.f32b, self.zT[:, sl], ALU.mult,
                           reads=[("f32b",), ("zT",)], writes=[("oT", hv)])
            if pr % 2 == 1:
                if self.dbg == "ret_g0" and pr == 1:
                    self.dma("sp", self.dbg1[:, :], self.oT[:, :, :].rearrange("p a s -> p (a s)"),
                             reads=[("oT", c) for c in range(4)])
                    self.dma("sp", self.dbg2[:, 0:S], QxT, reads=qx_all)
                    self.dma("sp", self.dbg2[:, S:2 * S], KT2, reads=[("KT2", a) for a in range(4)])
                    self.dma("sp", self.dbg2[:, 2 * S:3 * S], Kz.rearrange("p t k -> p (t k)"),
                             reads=[("Kz", t) for t in range(NT)])
                    self.dma("sp", self.dbg2[:, 3 * S:4 * S], Rb.rearrange("p t k -> p (t k)"),
                             reads=[("Rb", hh2, n) for hh2 in range(2) for n in range(1, NT)])
                    return
                self.out_proj(1, pr // 2, wout_d)
        self.mem_heads(1, ret_w, QMOFF, ZOFF, wkv_d, wout_d, QxT, qx_all)

    def final_norm(self, gb_d):
        outv = self.out_d.rearrange("(t p) d -> p t d", p=128)
        if self.n_layers < 2:
            for a in range(4):
                self.dma("sp", outv[:, 4 * a:4 * a + 4, :], self.xres[:, 4 * a:4 * a + 4, :],
                         reads=[("x", t) for t in range(4 * a, 4 * a + 4)])
            return
        gbt = self.f32ab
        self.dma("sp", gbt[:], gb_d[2], writes=[("f32a",), ("f32b",)])
        junk = self.zT
        for t in range(NT):
            src = self.xres[:, t, :]
            self.act(junk[:, 0:D], src, AF.Square, reads=[("x", t)], writes=[("zT",), ("ss", t)],
                     accum_out=self.ss_t[:, t:t + 1])
            self.act(self.rstd[:, t:t + 1], self.ss_t[:, t:t + 1], AF.Sqrt, reads=[("ss", t)], writes=[("rstd", t)],
                     scale=1.0 / D, bias=self.epsb[:, 0:1])
            self.v("dve", "reciprocal", self.rstd[:, t:t + 1], self.rstd[:, t:t + 1],
                   reads=[("rstd", t)], writes=[("rstd", t)])
            self.v("dve", "scalar_tensor_tensor",
                src, src, self.rstd[:, t:t + 1], gbt[:], ALU.mult, ALU.mult,
                reads=[("x", t), ("rstd", t), ("f32a",), ("f32b",)], writes=[("x", t)])
            if t % 4 == 3:
                a = t // 4
                self.dma("sp", outv[:, 4 * a:4 * a + 4, :], self.xres[:, 4 * a:4 * a + 4, :],
                         reads=[("x", tt) for tt in range(4 * a, 4 * a + 4)])


def build_nc(n_layers=2, dbg=None):
    from contextlib import ExitStack
    nc = bass.Bass("TRN2", target_bir_lowering=False)
    stack = ExitStack()
    b = Builder(nc, stack, n_layers=n_layers, dbg=dbg)
    with stack:
        b.build()
    return nc


def make_in_maps(x, mem, norm_g, fox_w_in, fox_b_f, ret_w_in, mem_norm_g, w_mem_kv, w_out, final_norm_g):
    f = lambda a: np.ascontiguousarray(np.asarray(a, dtype=np.float32))
    x, mem = f(x), f(mem)
    gb = np.stack([np.broadcast_to(f(norm_g)[0], (128, D)), np.broadcast_to(f(norm_g)[1], (128, D)),
                   np.broadcast_to(f(final_norm_g), (128, D)), np.broadcast_to(f(mem_norm_g), (128, D))])
    gb = np.ascontiguousarray(gb)
    bfb = np.ascontiguousarray(np.broadcast_to(np.tile(f(fox_b_f)[0], NT)[None, :], (128, NT * H)))
    shared = {"gb": gb, "bfb": bfb, "fox_w_in": f(fox_w_in)[0], "ret_w_in": f(ret_w_in)[0],
              "w_mem_kv": f(w_mem_kv), "w_out": f(w_out)}
    cs = host_consts()
    for name, shape, dt in CONST_SPECS:
        shared[name] = np.ascontiguousarray(cs[name])
    maps = []
    for b in range(x.shape[0]):
        m = dict(shared)
        m["x"] = x[b]
        m["mem"] = mem[b]
        maps.append(m)
    return maps


def kernel(x, mem, norm_g, fox_w_in, fox_b_f, ret_w_in, mem_norm_g, w_mem_kv, w_out, final_norm_g):
    nc = build_nc(2)
    in_maps = make_in_maps(x, mem, norm_g, fox_w_in, fox_b_f, ret_w_in, mem_norm_g, w_mem_kv, w_out, final_norm_g)
    res = run_bass_kernel_spmd(nc, in_maps, core_ids=list(range(8)))
    return np.stack([np.asarray(r["out"], dtype=np.float32) for r in res.results], axis=0)
```

```python
import math
import numpy as np
import ml_dtypes
import concourse.bass as bass
import concourse.mybir as mybir
from concourse.bass_utils import run_bass_kernel_spmd

F32 = mybir.dt.float32
BF16 = mybir.dt.bfloat16
AF = mybir.ActivationFunctionType
ALU = mybir.AluOpType

S = 2048
D = 1024
NT = 16
NQ = 4
H = 12
HM = 4
HD = 128
NMEM = 256
INNER = 2048
FOX_IN = 7180
RET_IN = 5632
EPS = 1e-6
SCALE = 1.0 / math.sqrt(HD)
SQ128 = math.sqrt(HD)
MASKNEG = -30000.0

ENGS = ("pe", "act", "dve", "pool", "sp")
N_DMA_SEMS = 40
ROWSUM_DVE = False


class Op:
    __slots__ = ("eng", "fn", "deps", "is_dma", "signal", "seq", "semidx", "semval", "prev_dma", "tag", "idx")


class Sched:
    def __init__(self):
        self.ops = {e: [] for e in ENGS}
        self.res = {}
        self.n_dma = 0
        self.dma_last = {}
        self.pending_dmas = []

    def add(self, eng, fn, reads=(), writes=(), dma=False, tag=None):
        op = Op()
        op.eng = eng
        op.fn = fn
        op.is_dma = dma
        op.signal = dma
        op.seq = None
        op.tag = tag
        op.prev_dma = None
        op.idx = len(self.ops[eng])
        deps = {}

        def need(p, kind):
            if p is op:
                return
            if p.is_dma or dma:
                deps[id(p)] = p
            elif p.eng == eng:
                if eng != "pe" and kind == "raw":
                    deps[id(p)] = p
            else:
                deps[id(p)] = p

        for r in reads:
            st = self.res.setdefault(r, [None, []])
            if st[0] is not None:
                need(st[0], "raw")
        for w in writes:
            st = self.res.setdefault(w, [None, []])
            if st[0] is not None:
                need(st[0], "waw")
            for rd in st[1]:
                need(rd, "war")
        for r in reads:
            self.res[r][1].append(op)
        for w in writes:
            st = self.res[w]
            st[0] = op
            st[1] = []
        latest = {}
        dl = []
        for p in deps.values():
            if p.is_dma:
                dl.append(p)
            else:
                q = latest.get(p.eng)
                if q is None or p.idx > q.idx:
                    latest[p.eng] = p
        op.deps = dl + list(latest.values())
        for p in op.deps:
            p.signal = True
        if dma:
            k = self.n_dma % N_DMA_SEMS
            op.semidx = k
            op.semval = 16 * (self.n_dma // N_DMA_SEMS + 1)
            op.prev_dma = self.dma_last.get(k)
            self.dma_last[k] = op
            self.n_dma += 1
            self.pending_dmas.append(op)
        self.ops[eng].append(op)
        return op

    def barrier(self):
        lasts = []
        for e in ENGS:
            for op in reversed(self.ops[e]):
                if not op.is_dma and op.fn is not None:
                    lasts.append(op)
                    break
        dmas = list(self.pending_dmas)
        for e in ENGS:
            op = Op()
            op.eng = e
            op.fn = None
            op.is_dma = False
            op.signal = False
            op.seq = None
            op.tag = "barrier"
            op.prev_dma = None
            op.idx = len(self.ops[e])
            op.deps = [p for p in lasts if p.eng != e] + dmas
            for p in op.deps:
                p.signal = True
            self.ops[e].append(op)
        self.res = {}
        self.pending_dmas = []

    def emit(self, nc, stack):
        eng_sems = {e: stack.enter_context(nc.semaphore("s_" + e)) for e in ENGS}
        dma_sems = [stack.enter_context(nc.semaphore("d_%d" % i)) for i in range(N_DMA_SEMS)]
        for e in ENGS:
            c = 0
            for op in self.ops[e]:
                if op.is_dma:
                    continue
                if op.signal:
                    c += 1
                    op.seq = c
        ops = self.ops

        def run(e, engine):
            waited = {}

            def wait_for(p):
                if p.is_dma:
                    key = ("d", p.semidx)
                    val = p.semval
                    sem = dma_sems[p.semidx]
                else:
                    key = ("e", p.eng)
                    val = p.seq
                    sem = eng_sems[p.eng]
                if waited.get(key, 0) < val:
                    engine.wait_ge(sem, val)
                    waited[key] = val

            for op in ops[e]:
                for p in op.deps:
                    wait_for(p)
                if op.prev_dma is not None:
                    wait_for(op.prev_dma)
                if op.fn is None:
                    continue
                inst = op.fn(engine)
                if op.is_dma:
                    inst.then_inc(dma_sems[op.semidx], 16)
                elif op.signal:
                    inst.then_inc(eng_sems[e], 1)

        block = stack.enter_context(nc.Block())

        @block.tensor
        def _(eng):
            run("pe", eng)

        @block.scalar
        def _(eng):
            run("act", eng)

        @block.vector
        def _(eng):
            run("dve", eng)

        @block.gpsimd
        def _(eng):
            run("pool", eng)

        @block.sync
        def _(eng):
            run("sp", eng)


def _bf(a):
    return np.asarray(a, dtype=np.float32).astype(ml_dtypes.bfloat16)


_CONSTS = None


def host_consts():
    global _CONSTS
    if _CONSTS is not None:
        return _CONSTS
    c = {}
    c["ident_bf"] = _bf(np.eye(128))
    c["ident_f"] = np.eye(128, dtype=np.float32)
    i = np.arange(128)
    c["tri_f"] = (i[:, None] <= i[None, :]).astype(np.float32)
    c["ones_f"] = np.ones((128, 128), np.float32)
    c["ones2_bf"] = _bf(np.full((128, 128), 2.0))
    c["onesg_bf"] = _bf(np.full((128, 128), 4.0 / 128.0))
    c["maskneg"] = _bf(np.where(i[:, None] <= i[None, :], 0.0, MASKNEG))
    sel = np.zeros((76, H, 128), np.float32)
    for h in range(H):
        sel[h, h, :] = 1.0
        sel[32 + h, h, :] = 1.0
        sel[64 + h, h, :] = 1.0
    c["sel"] = _bf(sel.reshape(76, H * 128))
    hh = np.arange(H, dtype=np.float64)
    lg = np.log1p(-np.exp2(-5.0 - hh))
    n = np.arange(128, dtype=np.float64)
    diff = n[None, :] - n[:, None]
    dm = np.where(diff[:, None, :] >= 0, np.exp(-lg[None, :, None] * (n[:, None, None] + 1.0)), 0.0)
    c["dmask"] = (dm * 0.125).astype(np.float32).reshape(128, H * 128)
    c["zeta"] = (0.125 * np.exp(lg[None, :] * (127.0 - n[:, None]))).astype(np.float32)
    c["xi"] = np.exp(lg[None, :] * (n[:, None] + 1.0)).astype(np.float32)
    c["gchunk"] = [float(np.exp(lg[h] * 128.0)) for h in range(H)]
    half = 32
    inv = 1.0 / (10000.0 ** (np.arange(half, dtype=np.float64) / half))
    pos = np.arange(S, dtype=np.float64)
    ang = pos[:, None] * inv[None, :]
    cos = np.cos(ang).reshape(NT, 128, half).transpose(1, 0, 2)
    sin = np.sin(ang).reshape(NT, 128, half).transpose(1, 0, 2)
    c["cc"] = np.concatenate([cos, cos], axis=-1).astype(np.float32).reshape(128, NT * 64)
    c["ss"] = np.concatenate([-sin, sin], axis=-1).astype(np.float32).reshape(128, NT * 64)
    _CONSTS = c
    return c


CONST_SPECS = [
    ("ident_bf", [128, 128], BF16), ("ident_f", [128, 128], F32), ("tri_f", [128, 128], F32),
    ("ones_f", [128, 128], F32), ("ones2_bf", [128, 128], BF16), ("onesg_bf", [128, 128], BF16),
    ("maskneg", [128, 128], BF16), ("sel", [76, H * 128], BF16),
    ("dmask", [128, H * 128], F32), ("zeta", [128, H], F32), ("xi", [128, H], F32),
    ("cc", [128, NT * 64], F32), ("ss", [128, NT * 64], F32),
]


class Builder:
    def __init__(self, nc, stack, n_layers=2, dbg=None):
        self.dbg = dbg
        self.nc = nc
        self.stack = stack
        self.sc = Sched()
        self.n_layers = n_layers
        self.evac_rr = 0
        self.ps_rr = 0
        self.scr_rr = 0

    def sb(self, name, shape, dt):
        return self.stack.enter_context(self.nc.sbuf_tensor(name, shape, dt))

    def dram_in(self, name, shape, dt):
        return self.nc.dram_tensor(name, shape, dt, kind="ExternalInput")

    def arena_reset(self):
        self.arena_off = 0

    def arena(self, nelem, dt):
        nbytes = nelem * (4 if dt == F32 else 2)
        nbytes = (nbytes + 63) // 64 * 64
        off = self.arena_off
        self.arena_off += nbytes
        assert self.arena_off <= self.ARENA_BYTES, (self.arena_off, self.ARENA_BYTES)
        if dt == F32:
            return self.arena_f[:, off // 4: off // 4 + nelem]
        return self.arena_b[:, off // 2: off // 2 + nelem]

    def dma(self, q, out, in_, reads=(), writes=()):
        return self.sc.add(q, lambda e: e.dma_start(out=out, in_=in_), reads, writes, dma=True)

    def mm(self, out, lhsT, rhs, start, stop, reads, writes):
        return self.sc.add("pe", lambda e: e.matmul(out, lhsT, rhs, start=start, stop=stop), reads, writes)

    def tr(self, out, in_, ident, reads, writes):
        return self.sc.add("pe", lambda e: e.transpose(out, in_, ident), reads, writes)

    def act(self, out, in_, func, reads, writes, bias=None, scale=None, accum_out=None):
        kw = {}
        if bias is not None:
            kw["bias"] = bias
        if scale is not None:
            kw["scale"] = scale
        if accum_out is not None:
            kw["accum_out"] = accum_out
        return self.sc.add("act", lambda e: e.activation(out, in_, func, **kw), reads, writes)

    def v(self, eng, name, *args, reads, writes):
        return self.sc.add(eng, lambda e: getattr(e, name)(*args), reads, writes)

    def copy(self, eng, out, in_, reads, writes):
        if eng == "act":
            return self.sc.add("act", lambda e: e.copy(out, in_), reads, writes)
        return self.sc.add(eng, lambda e: e.tensor_copy(out, in_), reads, writes)

    def evac_eng(self):
        self.evac_rr += 1
        return "dve" if (self.evac_rr % 2) else "act"

    def pbank(self):
        b = self.ps_rr % 2
        self.ps_rr += 1
        return b

    def build(self):
        nc = self.nc
        x_d = self.dram_in("x", [S, D], F32)
        mem_d = self.dram_in("mem", [NMEM, D], F32)
        gb_d = self.dram_in("gb", [4, 128, D], F32)
        bfb_d = self.dram_in("bfb", [128, NT * H], F32)
        fox_w = self.dram_in("fox_w_in", [D, FOX_IN], F32)
        ret_w = self.dram_in("ret_w_in", [D, RET_IN], F32)
        wkv_d = self.dram_in("w_mem_kv", [2, D, 2 * HM * HD], F32)
        wout_d = self.dram_in("w_out", [2, INNER, D], F32)
        self.cd = {}
        for name, shape, dt in CONST_SPECS:
            self.cd[name] = self.dram_in(name, shape, dt)
        self.out_d = nc.dram_tensor("out", [S, D], F32, kind="ExternalOutput")
        self.bfb_d = bfb_d
        if self.dbg:
            self.dbg1 = nc.dram_tensor("dbg1", [128, 4 * S], BF16, kind="ExternalOutput")
            self.dbg2 = nc.dram_tensor("dbg2", [128, 4 * S], BF16, kind="ExternalOutput")

        sb = self.sb
        self.xres = sb("xres", [128, NT, D], F32)
        self.hT = sb("hT", [128, 8, S], BF16)
        self.oT = sb("oT", [128, 4, S], BF16)
        self.Vg = sb("Vg", [128, NT, 512], BF16)
        self.zT = sb("zT", [128, S], BF16)
        self.wA = [sb("wA%d" % i, [128, 8, 128], BF16) for i in range(3)]
        self.wV = sb("wV", [128, 8, 512], BF16)
        self.wO = sb("wO", [128, 4, D], BF16)
        self.PT = [sb("PT%d" % i, [128, 512], BF16) for i in range(3)]
        self.f32ab = sb("f32ab", [128, D], F32)
        self.f32a = self.f32ab[:, 0:512]
        self.f32b = self.f32ab[:, 512:1024]
        self.scr = [sb("scr%d" % i, [128, D], BF16) for i in range(2)]
        self.ss_t = sb("ss_t", [128, NT], F32)
        self.rstd = sb("rstd", [128, NT], F32)
        self.epsb = sb("epsb", [128, 2], F32)
        self.memT = sb("memT", [128, 8, NMEM], BF16)
        self.KmT = sb("KmT", [128, HM, NMEM], BF16)
        self.Vm = sb("Vm", [128, 2, HM * HD], BF16)
        self.cst = {}
        for name in ("ident_bf", "ident_f", "ones2_bf", "onesg_bf"):
            spec = [s for s in CONST_SPECS if s[0] == name][0]
            self.cst[name] = sb("c_" + name, spec[1], spec[2])
        self.ARENA_BYTES = 33 * 1024
        self.arena_b = sb("arena", [128, self.ARENA_BYTES // 2], BF16)
        self.arena_f = self.arena_b.bitcast(F32)
        self.ps = [self.stack.enter_context(nc.psum_tensor("ps%d" % i, [128, 512], F32)) for i in range(8)]

        for name in ("ident_bf", "ident_f", "ones2_bf", "onesg_bf"):
            self.dma("sp", self.cst[name][:], self.cd[name][:], writes=[("c", name)])

        self.v("dve", "memset", self.epsb[:, 0:1], EPS, reads=[], writes=[("epsb",)])
        self.v("dve", "memset", self.epsb[:, 1:2], 4.0 * EPS, reads=[], writes=[("epsb",)])
        xv = x_d.rearrange("(t p) d -> p t d", p=128)
        for a in range(4):
            self.dma("sp", self.xres[:, 4 * a:4 * a + 4, :], xv[:, 4 * a:4 * a + 4, :],
                     writes=[("x", t) for t in range(4 * a, 4 * a + 4)])

        self.mem_norm(mem_d, gb_d)

        self.rms_to_hT(0, gb_d)
        self.fox_layer(fox_w, wkv_d, wout_d)
        if self.n_layers > 1:
            self.sc.barrier()
            self.rms_to_hT(1, gb_d)
            self.ret_layer(ret_w, wkv_d, wout_d)
        self.final_norm(gb_d)
        self.sc.barrier()
        self.sc.emit(nc, self.stack)

    def rms_rows(self, src_fn, nblk, g_idx, gb_d, dst_fn, src_res):
        gbt = self.f32ab
        self.dma("sp", gbt[:], gb_d[g_idx], writes=[("f32a",), ("f32b",)])
        junk = self.zT
        pb = self.ps[7].bitcast(BF16)
        for t in range(nblk):
            src = src_fn(t)
            scr = self.scr[self.scr_rr % 2]
            sres = ("scr", self.scr_rr % 2)
            self.scr_rr += 1
            self.act(junk[:, 0:D], src, AF.Square, reads=[src_res(t)], writes=[("zT",), ("ss", t)],
                     accum_out=self.ss_t[:, t:t + 1])
            self.act(self.rstd[:, t:t + 1], self.ss_t[:, t:t + 1], AF.Sqrt, reads=[("ss", t)], writes=[("rstd", t)],
                     scale=1.0 / D, bias=self.epsb[:, 0:1])
            self.v("dve", "reciprocal", self.rstd[:, t:t + 1], self.rstd[:, t:t + 1],
                   reads=[("rstd", t)], writes=[("rstd", t)])
            self.v("dve", "scalar_tensor_tensor",
                scr[:], src, self.rstd[:, t:t + 1], gbt[:], ALU.mult, ALU.mult,
                reads=[src_res(t), ("rstd", t), ("f32a",), ("f32b",)], writes=[sres])
            for c in range(8):
                self.tr(pb[:, c * 128:(c + 1) * 128], scr[:, c * 128:(c + 1) * 128],
                        self.cst["ident_bf"][:], reads=[sres, ("c", "ident_bf")], writes=[("ps", 7)])
            dst, dres = dst_fn(t)
            self.copy("act", dst, pb[:, 0:1024].rearrange("p (c k) -> p c k", c=8),
                      reads=[("ps", 7)], writes=[dres])

    def mem_norm(self, mem_d, gb_d):
        memv = mem_d.rearrange("(t p) d -> p t d", p=128)
        memraw = self.oT.bitcast(F32)[:, 0:2, :]
        ores = [("oT", c) for c in range(4)]
        self.dma("sp", memraw, memv, writes=ores)
        self.rms_rows(lambda t: memraw[:, t, :], 2, 3, gb_d,
                      lambda t: (self.memT[:, :, t * 128:(t + 1) * 128], ("memT",)),
                      lambda t: ores[0])

    def rms_to_hT(self, li, gb_d):
        self.rms_rows(lambda t: self.xres[:, t, :], NT, li, gb_d,
                      lambda t: (self.hT[:, :, t * 128:(t + 1) * 128], ("hT", t // 4)),
                      lambda t: ("x", t))

    def load_w(self, dst, w_d, col0, ncol, res):
        src = w_d.rearrange("(c p) n -> p c n", p=128)[:, :, col0:col0 + ncol]
        return self.dma("pool", dst, src, writes=[res] if not isinstance(res, list) else res)

    def proj_feat(self, wt, wres, post):
        for tb in range(NQ):
            bank = self.pbank()
            for c in range(8):
                self.mm(self.ps[bank][:, :], wt[:, c, :], self.hT[:, c, tb * 512:(tb + 1) * 512],
                        start=(c == 0), stop=(c == 7), reads=[wres, ("hT", tb)], writes=[("ps", bank)])
            post(tb, bank)

    def proj_tok(self, wt, wres, ncol, post):
        for t in range(NT):
            bank = self.pbank()
            for c in range(8):
                self.mm(self.ps[bank][:, 0:ncol], self.hT[:, c, t * 128:(t + 1) * 128], wt[:, c, 0:ncol],
                        start=(c == 0), stop=(c == 7), reads=[wres, ("hT", t // 4)], writes=[("ps", bank)])
            post(t, bank)

    def evac_bf16(self, dst, dres):
        def post(tb, bank):
            self.copy(self.evac_eng(), dst[:, tb * 512:(tb + 1) * 512], self.ps[bank][:, :],
                      reads=[("ps", bank)], writes=dres if isinstance(dres, list) else [dres])
        return post

    def evac_gate(self):
        def post(tb, bank):
            self.act(self.f32a, self.ps[bank][:, :], AF.Tanh, reads=[("ps", bank)], writes=[("f32a",)], scale=0.5)
            self.v("dve", "scalar_tensor_tensor", self.zT[:, tb * 512:(tb + 1) * 512], self.f32a, 1.0,
                                                           self.ps[bank][:, :], ALU.add, ALU.mult,
                   reads=[("f32a",), ("ps", bank)], writes=[("zT",)])
        return post

    def v_group(self, w_d, col0):
        self.load_w(self.wV[:], w_d, col0, 512, ("wV",))

        def post(t, bank):
            self.copy(self.evac_eng(), self.Vg[:, t, :], self.ps[bank][:, :],
                      reads=[("ps", bank)], writes=[("Vg", t)])
        self.proj_tok(self.wV, ("wV",), 512, post)

    def softmax_attn_tile(self, kt_ap, kt_res, q_sl, q_res, v_ap, v_res, col0, first, last,
                          bias_ap, bias_res, slot, extra=None):
        sbank = 2 + slot
        w = slice(col0, 512)
        qr = q_res if isinstance(q_res, list) else [q_res]
        self.mm(self.ps[sbank][:, w], kt_ap, q_sl, start=True, stop=(extra is None),
                reads=[kt_res] + qr, writes=[("ps", sbank)])
        if extra is not None:
            extra(sbank)
        kw = dict(scale=SCALE)
        if bias_ap is not None:
            kw["bias"] = bias_ap
        rd = [("ps", sbank)] + ([bias_res] if bias_res is not None else [])
        self.act(self.PT[slot][:, w], self.ps[sbank][:, w], AF.Exp, reads=rd, writes=[("PT", slot)], **kw)

        def pv():
            self.mm(self.ps[5][:, w], v_ap, self.PT[slot][:, w], start=first, stop=last,
                    reads=[v_res, ("PT", slot)], writes=[("ps", 5)])
            self.mm(self.ps[6][:, w], self.cst["ones2_bf"][:], self.PT[slot][:, w], start=first, stop=last,
                    reads=[("c", "ones2_bf"), ("PT", slot)], writes=[("ps", 6)])
        return pv

    def attn_finish(self, I, hh):
        sl = slice(I * 512, (I + 1) * 512)
        self.v("dve", "reciprocal", self.f32b, self.ps[6][:, :], reads=[("ps", 6)], writes=[("f32b",)])
        self.v("dve", "tensor_tensor", self.f32b, self.ps[5][:, :], self.f32b, ALU.mult,
               reads=[("ps", 5), ("f32b",)], writes=[("f32b",)])
        self.v("dve", "tensor_tensor", self.oT[:, hh, sl], self.f32b, self.zT[:, sl], ALU.mult,
               reads=[("f32b",), ("zT",)], writes=[("oT", hh)])

    def run_pipelined(self, tiles):
        pend = []
        for i, t in enumerate(tiles):
            pend.append(t(i % 3))
            if len(pend) > 2:
                pend.pop(0)()
        for p in pend:
            p()

    def out_proj(self, li, g, wout_d):
        src = wout_d[li, g * 512:(g + 1) * 512, :].rearrange("(c p) n -> p c n", p=128)
        self.dma("pool", self.wO[:], src, writes=[("wO",)])
        for t in range(NT):
            for half in range(2):
                bank = self.pbank()
                for c in range(4):
                    self.mm(self.ps[bank][:, :], self.oT[:, c, t * 128:(t + 1) * 128],
                            self.wO[:, c, half * 512:(half + 1) * 512], start=(c == 0), stop=(c == 3),
                            reads=[("oT", c), ("wO",)], writes=[("ps", bank)])
                xs = self.xres[:, t, half * 512:(half + 1) * 512]
                self.v("dve", "tensor_tensor", xs, xs, self.ps[bank][:, :], ALU.add,
                       reads=[("x", t), ("ps", bank)], writes=[("x", t)])

    def mem_kv(self, li, wkv_d):
        wrows = wkv_d[li]
        self.load_w(self.wV[:], wrows, 0, 512, ("wV",))
        for hm in range(HM):
            bank = self.pbank()
            for c in range(8):
                self.mm(self.ps[bank][:, 0:NMEM], self.wV[:, c, hm * 128:(hm + 1) * 128], self.memT[:, c, :],
                        start=(c == 0), stop=(c == 7), reads=[("wV",), ("memT",)], writes=[("ps", bank)])
            self.copy(self.evac_eng(), self.KmT[:, hm, :], self.ps[bank][:, 0:NMEM],
                      reads=[("ps", bank)], writes=[("KmT",)])
        self.load_w(self.wV[:], wrows, 512, 512, ("wV",))
        for mb in range(2):
            bank = self.pbank()
            for c in range(8):
                self.mm(self.ps[bank][:, :], self.memT[:, c, mb * 128:(mb + 1) * 128], self.wV[:, c, :],
                        start=(c == 0), stop=(c == 7), reads=[("wV",), ("memT",)], writes=[("ps", bank)])
            self.copy(self.evac_eng(), self.Vm[:, mb, :], self.ps[bank][:, :],
                      reads=[("ps", bank)], writes=[("Vm",)])

    def mem_heads(self, li, w_d, qm_col0, z_col0, wkv_d, wout_d, QT, qres):
        self.mem_kv(li, wkv_d)
        for hm in range(HM):
            wq, wz = self.wA[0], self.wA[2]
            self.load_w(wq[:], w_d, qm_col0 + hm * 128, 128, ("wA", 0))
            self.load_w(wz[:], w_d, z_col0 + (H + hm) * 128, 128, ("wA", 2))
            self.proj_feat(wq, ("wA", 0), self.evac_bf16(QT, qres))
            self.proj_feat(wz, ("wA", 2), self.evac_gate())
            for I in range(NQ):
                tiles = []
                for mb in range(2):
                    def mk(slot, mb=mb, I=I, hm=hm):
                        return self.softmax_attn_tile(
                            self.KmT[:, hm, mb * 128:(mb + 1) * 128], ("KmT",),
                            QT[:, I * 512:(I + 1) * 512], qres,
                            self.Vm[:, mb, hm * 128:(hm + 1) * 128], ("Vm",),
                            0, mb == 0, mb == 1, None, None, slot)
                    tiles.append(mk)
                self.run_pipelined(tiles)
                self.attn_finish(I, hm)
        self.out_proj(li, 3, wout_d)

    def fox_gate(self, fox_w):
        NH = NT * H
        self.load_w(self.wf, fox_w, 3 * H * HD, H, ("wf",))
        self.dma("sp", self.bfb, self.bfb_d[:], writes=[("bfb",)])
        for name in ("tri_f", "ones_f", "maskneg", "sel"):
            self.dma("sp", self.cst[name], self.cd[name][:], writes=[("c", name)])
        bank = 0
        for t in range(NT):
            for c in range(8):
                self.mm(self.ps[bank][:, t * H:(t + 1) * H], self.hT[:, c, t * 128:(t + 1) * 128], self.wf[:, c, :],
                        start=(c == 0), stop=(c == 7), reads=[("wf",), ("hT", t // 4)], writes=[("ps", bank)])
        nl, cn, off = self.nl, self.cn, self.off
        self.v("dve", "tensor_tensor", nl, self.ps[0][:, 0:NH], self.bfb, ALU.add,
               reads=[("ps", 0), ("bfb",)], writes=[("nl",)])
        self.act(nl, nl, AF.Exp, reads=[("nl",)], writes=[("nl",)], scale=-1.0)
        self.act(nl, nl, AF.Ln, reads=[("nl",)], writes=[("nl",)], bias=1.0)
        self.mm(self.ps[1][:, 0:NH], self.cst["tri_f"], nl, start=True, stop=True,
                reads=[("c", "tri_f"), ("nl",)], writes=[("ps", 1)])
        self.mm(self.ps[0][:, 0:NH], self.cst["ones_f"], nl, start=True, stop=True,
                reads=[("c", "ones_f"), ("nl",)], writes=[("ps", 0)])
        self.v("dve", "memset", off[:, 0:H], 0.0, reads=[], writes=[("off",)])
        for t in range(1, NT):
            self.v("dve", "tensor_tensor", off[:, t * H:(t + 1) * H], off[:, (t - 1) * H:t * H],
                                                         self.ps[0][:, (t - 1) * H:t * H], ALU.add,
                   reads=[("off",), ("ps", 0)], writes=[("off",)])
        self.v("dve", "tensor_tensor", cn, self.ps[1][:, 0:NH], off, ALU.add,
               reads=[("ps", 1), ("off",)], writes=[("cn",)])
        csp = self.csp
        csp3 = csp.rearrange("p (t k) -> p t k", k=76)
        vv, r1, hb = self.sp_v, self.sp_r, self.sp_hb
        self.v("dve", "memset", csp, 0.0, reads=[], writes=[("csp",)])
        self.v("dve", "tensor_scalar", vv, cn, -SQ128, None, ALU.mult, reads=[("cn",)], writes=[("spv",)])
        cur = vv
        for k, p0 in enumerate((0, 32, 64)):
            self.v("dve", "tensor_copy", hb, cur, reads=[("spv",), ("spr",)], writes=[("sphb",)])
            self.v("dve", "tensor_copy", csp3[:, :, p0:p0 + H], hb.rearrange("p (t h) -> p t h", h=H),
                   reads=[("sphb",)], writes=[("csp",)])
            if k < 2:
                nxt = r1 if cur is vv else vv
                self.v("dve", "tensor_tensor", nxt, cur, hb, ALU.subtract,
                       reads=[("spv",), ("spr",), ("sphb",)], writes=[("spv",), ("spr",)])
                cur = nxt
        for a in range(4):
            for tt in range(4):
                t = 4 * a + tt
                self.tr(self.ps[0][0:76, tt * 128:(tt + 1) * 128], csp3[:, t, :],
                        self.cst["ident_f"][:], reads=[("csp",), ("c", "ident_f")], writes=[("ps", 0)])
            self.v("dve", "tensor_copy", self.csplit[0:76, a * 512:(a + 1) * 512], self.ps[0][0:76, :],
                   reads=[("ps", 0)], writes=[("csplit",)])

    def fox_layer(self, fox_w, wkv_d, wout_d):
        QOFF, KOFF, VOFF, QMOFF, ZOFF = 0, H * HD, 2 * H * HD, 3 * H * HD + H, 3 * H * HD + H + HM * HD
        NH = NT * H
        self.arena_reset()
        QT = self.arena(S, BF16)
        KT = self.arena(S, BF16)
        self.csplit = self.arena(S, BF16)
        self.cst["sel"] = self.arena(H * 128, BF16)[0:76, :]
        self.cst["maskneg"] = self.arena(128, BF16)
        self.cst["tri_f"] = self.arena(128, F32)
        self.cst["ones_f"] = self.arena(128, F32)
        self.csp = self.arena(NT * 76, F32)
        self.sp_v = self.arena(NH, F32)
        self.sp_r = self.arena(NH, F32)
        self.sp_hb = self.arena(NH, BF16)
        self.bfb = self.arena(NH, F32)
        self.nl = self.arena(NH, F32)
        self.cn = self.arena(NH, F32)
        self.off = self.arena(NH, F32)
        self.wf = self.arena(8 * H, BF16).rearrange("p (c h) -> p c h", h=H)

        self.fox_gate(fox_w)
        for g in range(3):
            self.v_group(fox_w, VOFF + g * 512)
            for hh in range(4):
                h = 4 * g + hh
                wq, wk, wz = self.wA
                self.load_w(wq[:], fox_w, QOFF + h * 128, 128, ("wA", 0))
                self.load_w(wk[:], fox_w, KOFF + h * 128, 128, ("wA", 1))
                self.load_w(wz[:], fox_w, ZOFF + h * 128, 128, ("wA", 2))
                self.proj_feat(wq, ("wA", 0), self.evac_bf16(QT, ("QT",)))
                self.proj_feat(wk, ("wA", 1), self.evac_bf16(KT, ("KT",)))
                self.proj_feat(wz, ("wA", 2), self.evac_gate())
                for I in range(NQ):
                    tiles = []
                    nj = 4 * I + 4
                    for j in range(nj):
                        m = j - 4 * I
                        col0 = 128 * m if m >= 0 else 0

                        def mk(slot, j=j, m=m, col0=col0, I=I, nj=nj, h=h, hh=hh):
                            def extra(sbank):
                                if m >= 0:
                                    self.mm(self.ps[sbank][:, col0:col0 + 128], self.cst["ident_bf"][:],
                                            self.cst["maskneg"], start=False, stop=False,
                                            reads=[("c", "ident_bf"), ("c", "maskneg")], writes=[("ps", sbank)])
                                self.mm(self.ps[sbank][:, col0:512], self.cst["sel"][:, h * 128:(h + 1) * 128],
                                        self.csplit[0:76, I * 512 + col0:(I + 1) * 512], start=False, stop=True,
                                        reads=[("c", "sel"), ("csplit",)], writes=[("ps", sbank)])
                            return self.softmax_attn_tile(
                                KT[:, j * 128:(j + 1) * 128], ("KT",),
                                QT[:, I * 512 + col0:(I + 1) * 512], ("QT",),
                                self.Vg[:, j, hh * 128:(hh + 1) * 128], ("Vg", j),
                                col0, j == 0, j == nj - 1,
                                self.cn[:, j * H + h:j * H + h + 1], ("cn",), slot, extra=extra)
                        tiles.append(mk)
                    self.run_pipelined(tiles)
                    self.attn_finish(I, hh)
            self.out_proj(0, g, wout_d)
        self.mem_heads(0, fox_w, QMOFF, ZOFF, wkv_d, wout_d, QT, ("QT",))

    def ret_layer(self, ret_w, wkv_d, wout_d):
        QOFF, KOFF, VOFF, QMOFF, ZOFF = 0, H * 64, 2 * H * 64, 2 * H * 64 + H * HD, 2 * H * 64 + H * HD + HM * HD
        gch = host_consts()["gchunk"]
        self.arena_reset()
        QxT = self.arena(S, BF16)
        KT2 = self.arena(S, BF16)
        Kz = self.arena(NT * 128, BF16).rearrange("p (t k) -> p t k", k=128)
        Rb = self.arena(NT * 128, BF16).rearrange("p (t k) -> p t k", k=128)
        Rf = self.arena(128, F32)
        cc = self.arena(NT * 64, F32).rearrange("p (t k) -> p t k", k=64)
        ss = self.arena(NT * 64, F32).rearrange("p (t k) -> p t k", k=64)
        zeta = self.arena(H, F32)
        xi = self.arena(H, F32)
        dmk = [self.arena(128, F32) for _ in range(2)]
        ra = self.arena(256, F32)
        rb = self.arena(256, F32)
        qkb = self.arena(256, BF16)
        wqk = self.arena(8 * 256, BF16).rearrange("p (c k) -> p c k", k=256)
        self.dma("sp", cc, self.cd["cc"].rearrange("p (t k) -> p t k", k=64), writes=[("cc",)])
        self.dma("sp", ss, self.cd["ss"].rearrange("p (t k) -> p t k", k=64), writes=[("ss",)])
        self.dma("sp", zeta, self.cd["zeta"][:], writes=[("zeta",)])
        self.dma("sp", xi, self.cd["xi"][:], writes=[("xi",)])
        pb7 = self.ps[7].bitcast(BF16)
        ra4 = ra.rearrange("p (a k) -> p a k", k=64)
        rb4 = rb.rearrange("p (a k) -> p a k", k=64)
        wsrc = ret_w.rearrange("(c p) n -> p c n", p=128)
        qx_all = [("QxT", a) for a in range(4)]

        for pr in range(H // 2):
            h0 = 2 * pr
            if pr % 2 == 0:
                self.v_group(ret_w, VOFF + (pr // 2) * 512)
            self.dma("pool", wqk[:, :, 0:128], wsrc[:, :, QOFF + h0 * 64:QOFF + h0 * 64 + 128], writes=[("wqk",)])
            self.dma("pool", wqk[:, :, 128:256], wsrc[:, :, KOFF + h0 * 64:KOFF + h0 * 64 + 128], writes=[("wqk",)])

            def post(t, bank, h0=h0):
                p4 = self.ps[bank][:, 0:256].rearrange("p (a k) -> p a k", k=64)
                self.v("dve", "tensor_tensor", ra4, p4, cc[:, t:t + 1, :].broadcast_to([128, 4, 64]), ALU.mult,
                       reads=[("ps", bank), ("cc",)], writes=[("ra",)])
                self.v("dve", "tensor_tensor", rb4[:, :, 0:32], p4[:, :, 32:64],
                       ss[:, t:t + 1, 0:32].broadcast_to([128, 4, 32]), ALU.mult,
                       reads=[("ps", bank), ("ss",)], writes=[("rb",)])
                self.v("dve", "tensor_tensor", rb4[:, :, 32:64], p4[:, :, 0:32],
                       ss[:, t:t + 1, 32:64].broadcast_to([128, 4, 32]), ALU.mult,
                       reads=[("ps", bank), ("ss",)], writes=[("rb",)])
                self.v("dve", "tensor_tensor", ra, ra, rb, ALU.add, reads=[("ra",), ("rb",)], writes=[("ra",)])
                self.v("dve", "tensor_tensor", qkb[:, 0:128].rearrange("p (a k) -> p a k", k=64), ra4[:, 0:2, :],
                       xi[:, h0:h0 + 2].unsqueeze(2).to_broadcast([128, 2, 64]), ALU.mult,
                       reads=[("ra",), ("xi",)], writes=[("qkb",)])
                self.v("dve", "tensor_copy", qkb[:, 128:256], ra[:, 128:256], reads=[("ra",)], writes=[("qkb",)])
                self.v("dve", "tensor_tensor", Kz[:, t, :].rearrange("p (a k) -> p a k", k=64), ra4[:, 2:4, :],
                       zeta[:, h0:h0 + 2].unsqueeze(2).to_broadcast([128, 2, 64]), ALU.mult,
                       reads=[("ra",), ("zeta",)], writes=[("Kz", t)])
                tt = t % 4
                self.tr(pb7[:, (2 * tt) * 128:(2 * tt + 1) * 128], qkb[:, 0:128], self.cst["ident_bf"][:],
                        reads=[("qkb",), ("c", "ident_bf")], writes=[("ps", 7)])
                self.tr(pb7[:, (2 * tt + 1) * 128:(2 * tt + 2) * 128], qkb[:, 128:256], self.cst["ident_bf"][:],
                        reads=[("qkb",), ("c", "ident_bf")], writes=[("ps", 7)])
                if tt == 3:
                    a = t // 4
                    pv4 = pb7[:, 0:1024].rearrange("p (t w k) -> p t w k", w=2, k=128)
                    self.copy("act", QxT[:, a * 512:(a + 1) * 512].rearrange("p (t k) -> p t k", k=128), pv4[:, :, 0, :],
                              reads=[("ps", 7)], writes=[("QxT", a)])
                    self.copy("act", KT2[:, a * 512:(a + 1) * 512].rearrange("p (t k) -> p t k", k=128), pv4[:, :, 1, :],
                              reads=[("ps", 7)], writes=[("KT2", a)])
            self.proj_tok(wqk, ("wqk",), 256, post)

            for hh2 in range(2):
                h = h0 + hh2
                hv = h % 4
                rows = slice(64 * hh2, 64 * hh2 + 64)
                for a in range(4):
                    bank = self.pbank()
                    ns = [n for n in range(4 * a, 4 * a + 4) if n < NT - 1]
                    for n in ns:
                        self.mm(self.ps[bank][:, (n % 4) * 128:(n % 4 + 1) * 128], Kz[:, n, :],
                                self.Vg[:, n, hv * 128:(hv + 1) * 128], start=True, stop=True,
                                reads=[("Kz", n), ("Vg", n)], writes=[("ps", bank)])
                    for n in ns:
                        u = self.ps[bank][rows, (n % 4) * 128:(n % 4 + 1) * 128]
                        if n == 0:
                            self.v("dve", "tensor_copy", Rf[rows, :], u, reads=[("ps", bank)], writes=[("Rf", hh2)])
                        else:
                            self.v("dve", "scalar_tensor_tensor", Rf[rows, :], Rf[rows, :], gch[h], u,
                                   ALU.mult, ALU.add, reads=[("ps", bank), ("Rf", hh2)], writes=[("Rf", hh2)])
                        self.v("dve", "tensor_copy", Rb[rows, n + 1, :], Rf[rows, :],
                               reads=[("Rf", hh2)], writes=[("Rb", hh2, n + 1)])

            for hh2 in range(2):
                h = h0 + hh2
                hv = h % 4
                rows = slice(64 * hh2, 64 * hh2 + 64)
                self.load_w(self.wA[2][:], ret_w, ZOFF + h * 128, 128, ("wA", 2))
                self.proj_feat(self.wA[2], ("wA", 2), self.evac_gate())
                dm = dmk[h % 2]
                self.dma("sp", dm, self.cd["dmask"][:, h * 128:(h + 1) * 128], writes=[("dmk", h % 2)])
                for I in range(NQ):
                    slot = I % 3
                    sbank = 2 + slot
                    for nn in range(4):
                        n = 4 * I + nn
                        self.mm(self.ps[sbank][:, nn * 128:(nn + 1) * 128], KT2[rows, n * 128:(n + 1) * 128],
                                QxT[rows, n * 128:(n + 1) * 128], start=True, stop=True,
                                reads=[("KT2", I), ("QxT", I)], writes=[("ps", sbank)])
                    Am = self.PT[slot]
                    self.v("dve", "tensor_tensor", Am.rearrange("p (a k) -> p a k", k=128),
                           self.ps[sbank][:, :].rearrange("p (a k) -> p a k", k=128),
                           dm.unsqueeze(1).to_broadcast([128, 4, 128]), ALU.mult,
                           reads=[("ps", sbank), ("dmk", h % 2)], writes=[("PT", slot)])
                    for nn in range(4):
                        n = 4 * I + nn
                        self.mm(self.ps[5][:, nn * 128:(nn + 1) * 128], self.Vg[:, n, hv * 128:(hv + 1) * 128],
                                Am[:, nn * 128:(nn + 1) * 128], start=True, stop=(n == 0),
                                reads=[("Vg", n), ("PT", slot)], writes=[("ps", 5)])
                        if n > 0:
                            self.mm(self.ps[5][:, nn * 128:(nn + 1) * 128], Rb[rows, n, :],
                                    QxT[rows, n * 128:(n + 1) * 128], start=False, stop=True,
                                    reads=[("Rb", hh2, n), ("QxT", I)], writes=[("ps", 5)])
                    s2 = (slot + 1) % 3
                    sq = self.PT[s2]
                    self.act(sq[:, :], self.ps[5][:, :], AF.Square, reads=[("ps", 5)], writes=[("PT", s2)])
                    self.copy("act", self.f32a, self.ps[5][:, :], reads=[("ps", 5)], writes=[("f32a",)])
                    self.mm(self.ps[6][:, :], self.cst["onesg_bf"][:], sq[:, :], start=True, stop=True,
                            reads=[("c", "onesg_bf"), ("PT", s2)], writes=[("ps", 6)])
                    self.act(self.f32b, self.ps[6][:, :], AF.Sqrt, reads=[("ps", 6)], writes=[("f32b",)],
                             bias=self.epsb[:, 1:2])
                    self.v("dve", "reciprocal", self.f32b, self.f32b, reads=[("f32b",)], writes=[("f32b",)])
                    self.v("dve", "tensor_tensor", self.f32b, self.f32b, self.f32a, ALU.mult,
                           reads=[("f32a",), ("f32b",)], writes=[("f32b",)])
                    sl = slice(I * 512, (I + 1) * 512)
                    self.v("dve", "tensor_tensor", self.oT[:, hv, sl], self.f32b, self.zT[:, sl], ALU.mult,
                           reads=[("f32b",), ("zT",)], writes=[("oT", hv)])
            if pr % 2 == 1:
                if self.dbg == "ret_g0" and pr == 1:
                    self.dma("sp", self.dbg1[:, :], self.oT[:, :, :].rearrange("p a s -> p (a s)"),
                             reads=[("oT", c) for c in range(4)])
                    self.dma("sp", self.dbg2[:, 0:S], QxT, reads=qx_all)
                    self.dma("sp", self.dbg2[:, S:2 * S], KT2, reads=[("KT2", a) for a in range(4)])
                    self.dma("sp", self.dbg2[:, 2 * S:3 * S], Kz.rearrange("p t k -> p (t k)"),
                             reads=[("Kz", t) for t in range(NT)])
                    self.dma("sp", self.dbg2[:, 3 * S:4 * S], Rb.rearrange("p t k -> p (t k)"),
                             reads=[("Rb", hh2, n) for hh2 in range(2) for n in range(1, NT)])
                    return
                self.out_proj(1, pr // 2, wout_d)
        self.mem_heads(1, ret_w, QMOFF, ZOFF, wkv_d, wout_d, QxT, qx_all)

    def final_norm(self, gb_d):
        outv = self.out_d.rearrange("(t p) d -> p t d", p=128)
        if self.n_layers < 2:
            for a in range(4):
                self.dma("sp", outv[:, 4 * a:4 * a + 4, :], self.xres[:, 4 * a:4 * a + 4, :],
                         reads=[("x", t) for t in range(4 * a, 4 * a + 4)])
            return
        gbt = self.f32ab
        self.dma("sp", gbt[:], gb_d[2], writes=[("f32a",), ("f32b",)])
        junk = self.zT
        for t in range(NT):
            src = self.xres[:, t, :]
            self.act(junk[:, 0:D], src, AF.Square, reads=[("x", t)], writes=[("zT",), ("ss", t)],
                     accum_out=self.ss_t[:, t:t + 1])
            self.act(self.rstd[:, t:t + 1], self.ss_t[:, t:t + 1], AF.Sqrt, reads=[("ss", t)], writes=[("rstd", t)],
                     scale=1.0 / D, bias=self.epsb[:, 0:1])
            self.v("dve", "reciprocal", self.rstd[:, t:t + 1], self.rstd[:, t:t + 1],
                   reads=[("rstd", t)], writes=[("rstd", t)])
            self.v("dve", "scalar_tensor_tensor",
                src, src, self.rstd[:, t:t + 1], gbt[:], ALU.mult, ALU.mult,
                reads=[("x", t), ("rstd", t), ("f32a",), ("f32b",)], writes=[("x", t)])
            if t % 4 == 3:
                a = t // 4
                self.dma("sp", outv[:, 4 * a:4 * a + 4, :], self.xres[:, 4 * a:4 * a + 4, :],
                         reads=[("x", tt) for tt in range(4 * a, 4 * a + 4)])


def build_nc(n_layers=2, dbg=None):
    from contextlib import ExitStack
    nc = bass.Bass("TRN2", target_bir_lowering=False)
    stack = ExitStack()
    b = Builder(nc, stack, n_layers=n_layers, dbg=dbg)
    with stack:
        b.build()
    return nc


def make_in_maps(x, mem, norm_g, fox_w_in, fox_b_f, ret_w_in, mem_norm_g, w_mem_kv, w_out, final_norm_g):
    f = lambda a: np.ascontiguousarray(np.asarray(a, dtype=np.float32))
    x, mem = f(x), f(mem)
    gb = np.stack([np.broadcast_to(f(norm_g)[0], (128, D)), np.broadcast_to(f(norm_g)[1], (128, D)),
                   np.broadcast_to(f(final_norm_g), (128, D)), np.broadcast_to(f(mem_norm_g), (128, D))])
    gb = np.ascontiguousarray(gb)
    bfb = np.ascontiguousarray(np.broadcast_to(np.tile(f(fox_b_f)[0], NT)[None, :], (128, NT * H)))
    shared = {"gb": gb, "bfb": bfb, "fox_w_in": f(fox_w_in)[0], "ret_w_in": f(ret_w_in)[0],
              "w_mem_kv": f(w_mem_kv), "w_out": f(w_out)}
    cs = host_consts()
    for name, shape, dt in CONST_SPECS:
        shared[name] = np.ascontiguousarray(cs[name])
    maps = []
    for b in range(x.shape[0]):
        m = dict(shared)
        m["x"] = x[b]
        m["mem"] = mem[b]
        maps.append(m)
    return maps


def kernel(x, mem, norm_g, fox_w_in, fox_b_f, ret_w_in, mem_norm_g, w_mem_kv, w_out, final_norm_g):
    nc = build_nc(2)
    in_maps = make_in_maps(x, mem, norm_g, fox_w_in, fox_b_f, ret_w_in, mem_norm_g, w_mem_kv, w_out, final_norm_g)
    res = run_bass_kernel_spmd(nc, in_maps, core_ids=list(range(8)))
    return np.stack([np.asarray(r["out"], dtype=np.float32) for r in res.results], axis=0)
```
